# Optimizing a Trainium2 kernel written in Bass

```python
import math
import jax, jax.numpy as jnp
from jax import lax
import numpy as np

D_MODEL = 1024
BATCH = 4
SEQ = 4096
DEPTH = 2
DEC_BATCH = 4
DEC_SEQ = 8192
PAST_LEN = 128

MIX_WIDTH = D_MODEL
HEAD_DIM = 64
N_HEADS = 8
N_KV = 2
GROUP = N_HEADS // N_KV
ATTN_WIDTH = N_HEADS * HEAD_DIM
CONV_CH = MIX_WIDTH - ATTN_WIDTH
CONV_K = 31
WINDOW = 128
BLOCK = 128
N_BUCKETS = 32
MAX_DISTANCE = 128
D_FF = 2816
FFN_CONV_K = 3
PROJ_WIDTH = ATTN_WIDTH + 2 * N_KV * HEAD_DIM + 2 * CONV_CH
EPS = 1e-6
NEG = -1e30

kernel_name = "hybrid_bidir_window_gqa_conformer_convffn"


def _rmsnorm(x, g):
    x32 = x.astype(jnp.float32)
    y = x32 * lax.rsqrt(jnp.mean(x32 * x32, axis=-1, keepdims=True) + EPS)
    return (y * g.astype(jnp.float32)).astype(x.dtype)


def _layernorm(x, g, b):
    x32 = x.astype(jnp.float32)
    mu = jnp.mean(x32, axis=-1, keepdims=True)
    xc = x32 - mu
    y = xc * lax.rsqrt(jnp.mean(xc * xc, axis=-1, keepdims=True) + EPS)
    return (y * g.astype(jnp.float32) + b.astype(jnp.float32)).astype(x.dtype)


def _dwconv(x, w, b):
    k = w.shape[0]
    pad = (k - 1) // 2
    y = lax.conv_general_dilated(
        x, w[:, None, :].astype(x.dtype), window_strides=(1,), padding=[(pad, pad)],
        dimension_numbers=("NWC", "WIO", "NWC"), feature_group_count=x.shape[-1])
    return y + b.astype(x.dtype)


def _t5_buckets(rel):
    nb = N_BUCKETS // 2
    ret = (rel > 0).astype(jnp.int32) * nb
    n = jnp.abs(rel)
    max_exact = nb // 2
    nf = jnp.maximum(n, 1).astype(jnp.float32)
    large = max_exact + (jnp.log(nf / max_exact) / math.log(MAX_DISTANCE / max_exact)
                         * (nb - max_exact)).astype(jnp.int32)
    large = jnp.minimum(large, nb - 1)
    return ret + jnp.where(n < max_exact, n, large)


def _band_bias(rel_bias):
    qi = jnp.arange(BLOCK, dtype=jnp.int32)[:, None]
    kj = jnp.arange(3 * BLOCK, dtype=jnp.int32)[None, :] - BLOCK
    rel = kj - qi
    vals = rel_bias.astype(jnp.float32)[_t5_buckets(rel)]
    vals = jnp.transpose(vals, (2, 0, 1)).reshape(N_KV, GROUP, BLOCK, 3 * BLOCK)
    band = jnp.abs(rel) <= WINDOW
    return vals, band


def _band_blocks(t, nb):
    b = t.shape[0]
    tp = jnp.pad(t, ((0, 0), (BLOCK, BLOCK), (0, 0), (0, 0))).reshape(b, nb + 2, BLOCK, N_KV, HEAD_DIM)
    return jnp.concatenate([tp[:, :-2], tp[:, 1:-1], tp[:, 2:]], axis=2)


def _window_attn(q, k, v, bias, band, sink):
    b, s, _ = q.shape
    nb = s // BLOCK
    qb = q.reshape(b, nb, BLOCK, N_KV, GROUP, HEAD_DIM)
    kb = _band_blocks(k.reshape(b, s, N_KV, HEAD_DIM), nb)
    vb = _band_blocks(v.reshape(b, s, N_KV, HEAD_DIM), nb)
    kpos = jnp.arange(nb)[:, None] * BLOCK + jnp.arange(3 * BLOCK)[None, :] - BLOCK
    kvalid = (kpos >= 0) & (kpos < s)
    mask = band[None] & kvalid[:, None, :]
    sc = jnp.einsum("bnqhgd,bnkhd->bnhgqk", qb, kb,
                    preferred_element_type=jnp.float32) * (HEAD_DIM ** -0.5)
    sc = sc + bias[None, None]
    sc = jnp.where(mask[None, :, None, None], sc, NEG)
    sk = sink.astype(jnp.float32).reshape(N_KV, GROUP, 1, 1)
    m = jnp.maximum(jnp.max(sc, axis=-1, keepdims=True), sk)
    p = jnp.exp(sc - m)
    denom = jnp.sum(p, axis=-1, keepdims=True) + jnp.exp(sk - m)
    p = (p / denom).astype(v.dtype)
    o = jnp.einsum("bnhgqk,bnkhd->bnqhgd", p, vb)
    return o.reshape(b, s, ATTN_WIDTH)


def _trunk(x, bias, band, norm_attn_g, w_in, attn_sink, conv_dw_w, conv_dw_b, conv_ln_g,
           conv_ln_b, w_out, norm_ffn_g, w_up, ffn_dw_w, ffn_dw_b, w_down, norm_final_g):
    kv_w = N_KV * HEAD_DIM
    for l in range(DEPTH):
        h = _rmsnorm(x, norm_attn_g[l])
        z = h @ w_in[l]
        q = z[..., :ATTN_WIDTH]
        k = z[..., ATTN_WIDTH:ATTN_WIDTH + kv_w]
        v = z[..., ATTN_WIDTH + kv_w:ATTN_WIDTH + 2 * kv_w]
        c0 = ATTN_WIDTH + 2 * kv_w
        ca = z[..., c0:c0 + CONV_CH]
        cb = z[..., c0 + CONV_CH:]
        a_out = _window_attn(q, k, v, bias, band, attn_sink[l])
        c = ca * jax.nn.sigmoid(cb)
        c = _dwconv(c, conv_dw_w[l], conv_dw_b[l])
        c = jax.nn.silu(_layernorm(c, conv_ln_g[l], conv_ln_b[l]))
        x = x + jnp.concatenate([a_out, c], axis=-1) @ w_out[l]
        h = _rmsnorm(x, norm_ffn_g[l])
        u = _dwconv(h @ w_up[l], ffn_dw_w[l], ffn_dw_b[l])
        x = x + (jax.nn.silu(u[..., :D_FF]) * u[..., D_FF:]) @ w_down[l]
    return _rmsnorm(x, norm_final_g)


def setup_inputs(seed: int = 0) -> dict:
    key = jax.random.key(seed)
    ks = jax.random.split(key, 20)
    f32 = jnp.float32
    nrm = lambda k, shape, scale: jax.random.normal(k, shape, f32) * scale
    return {
        "x_prompt": nrm(ks[0], (BATCH, SEQ, D_MODEL), 1.0),
        "x_sample": nrm(ks[1], (DEC_BATCH, DEC_SEQ, D_MODEL), 1.0),
        "rel_bias": nrm(ks[2], (N_BUCKETS, N_HEADS), 0.5),
        "norm_attn_g": 1.0 + nrm(ks[3], (DEPTH, D_MODEL), 0.02),
        "w_in": nrm(ks[4], (DEPTH, D_MODEL, PROJ_WIDTH), D_MODEL ** -0.5),
        "attn_sink": nrm(ks[5], (DEPTH, N_HEADS), 0.5),
        "conv_dw_w": nrm(ks[6], (DEPTH, CONV_K, CONV_CH), CONV_K ** -0.5),
        "conv_dw_b": nrm(ks[7], (DEPTH, CONV_CH), 0.02),
        "conv_ln_g": 1.0 + nrm(ks[8], (DEPTH, CONV_CH), 0.02),
        "conv_ln_b": nrm(ks[9], (DEPTH, CONV_CH), 0.02),
        "w_out": nrm(ks[10], (DEPTH, MIX_WIDTH, D_MODEL), MIX_WIDTH ** -0.5),
        "norm_ffn_g": 1.0 + nrm(ks[11], (DEPTH, D_MODEL), 0.02),
        "w_up": nrm(ks[12], (DEPTH, D_MODEL, 2 * D_FF), D_MODEL ** -0.5),
        "ffn_dw_w": nrm(ks[13], (DEPTH, FFN_CONV_K, 2 * D_FF), FFN_CONV_K ** -0.5),
        "ffn_dw_b": nrm(ks[14], (DEPTH, 2 * D_FF), 0.02),
        "w_down": nrm(ks[15], (DEPTH, D_FF, D_MODEL), D_FF ** -0.5),
        "norm_final_g": 1.0 + nrm(ks[16], (D_MODEL,), 0.02),
    }


def reference(x_prompt, x_sample, rel_bias, norm_attn_g, w_in, attn_sink, conv_dw_w, conv_dw_b,
              conv_ln_g, conv_ln_b, w_out, norm_ffn_g, w_up, ffn_dw_w, ffn_dw_b, w_down,
              norm_final_g):
    bias, band = _band_bias(rel_bias)
    y_prompt = _trunk(x_prompt, bias, band, norm_attn_g, w_in, attn_sink, conv_dw_w, conv_dw_b,
                      conv_ln_g, conv_ln_b, w_out, norm_ffn_g, w_up, ffn_dw_w, ffn_dw_b, w_down,
                      norm_final_g)
    y_sample = _trunk(x_sample, bias, band, norm_attn_g, w_in, attn_sink, conv_dw_w, conv_dw_b,
                      conv_ln_g, conv_ln_b, w_out, norm_ffn_g, w_up, ffn_dw_w, ffn_dw_b, w_down,
                      norm_final_g)
    return (y_prompt, y_sample)
```

```python
import numpy as np
import concourse.bass as bass
import concourse.mybir as mybir
from concourse.bass_utils import run_bass_kernel_spmd

F32 = mybir.dt.float32
BF16 = mybir.dt.bfloat16
AF = mybir.ActivationFunctionType
ALU = mybir.AluOpType

D = 1024
DFF = 2816
NFF = 22
PW = 1792
OWN = 1024
HALO = 258
WMAX = OWN + 2 * HALO
EPS = 1e-6
NEG = -30000.0
LP = 2048
LS = 4096
NTOT = 53200
FF_GROUPS = [(0, 5), (5, 5), (10, 4), (14, 4), (18, 4)]
GMAX = 5

C_GA = 0
C_GF = 16
C_GN = 32
C_CB = 40
C_LG = 48
C_LB = 56
C_CW = 64
C_FW = 312
C_FB = 576
NCST = 664


class Trk:
    def __init__(self, nc, dram_names, psum_names):
        self.nc = nc
        self.eng = {'pe': nc.tensor, 'act': nc.scalar, 'dve': nc.vector, 'pool': nc.gpsimd, 'sp': nc.sync}
        self.sem = {}
        self.cnt = {}
        for e in ['pe', 'act', 'dve', 'pool']:
            self.sem[e] = nc.alloc_semaphore('s_' + e)
            self.cnt[e] = 0
        self.waited = {}
        self.lastw = {}
        self.readers = {}
        self.dram_names = dram_names
        self.psum_names = psum_names
        self.ninst = 0

    def units(self, ap):
        name = ap.tensor.name
        if name in self.psum_names:
            return [('ps', name)]
        if name in self.dram_names:
            return []
        es = 4 if ap.dtype == F32 else 2
        col = (ap.offset * es) % (NTOT * 4)
        ext = 1
        for st, cn in ap.ap[1:]:
            ext += (cn - 1) * abs(st)
        lo = col
        hi = col + ext * es
        return [(name, u) for u in range(lo // 512, (hi - 1) // 512 + 1)]

    def _need(self, e, ru, wu):
        need = {}
        for u in ru:
            lw = self.lastw.get(u)
            if lw:
                need[lw[0]] = max(need.get(lw[0], 0), lw[1])
        for u in wu:
            lw = self.lastw.get(u)
            if lw and lw[0] != e:
                need[lw[0]] = max(need.get(lw[0], 0), lw[1])
            for p, c in self.readers.get(u, {}).items():
                if p != e:
                    need[p] = max(need.get(p, 0), c)
        for p, c in need.items():
            if self.waited.get((e, p), 0) < c:
                self.eng[e].wait_ge(self.sem[p], c)
                self.waited[(e, p)] = c

    def issue(self, e, fn, reads=(), writes=(), rkeys=(), wkeys=()):
        ru = [u for a in reads for u in self.units(a)] + list(rkeys)
        wu = [u for a in writes for u in self.units(a)] + list(wkeys)
        self._need(e, ru, wu)
        inst = fn()
        self.cnt[e] += 1
        c = self.cnt[e]
        inst.then_inc(self.sem[e], 1)
        self.ninst += 1
        for u in ru:
            self.readers.setdefault(u, {})[e] = c
        for u in wu:
            self.lastw[u] = (e, c)
            self.readers[u] = {}

    def dma(self, q, key, pairs, rkeys=(), wkeys=()):
        if key not in self.sem:
            self.sem[key] = self.nc.alloc_semaphore('d_' + str(len(self.sem)))
            self.cnt[key] = 0
        ru = list(rkeys)
        wu = list(wkeys)
        for o, i in pairs:
            ru += self.units(i)
            wu += self.units(o)
        self._need(q, ru, wu)
        for o, i in pairs:
            self.eng[q].dma_start(out=o, in_=i).then_inc(self.sem[key], 16)
            self.cnt[key] += 16
            self.ninst += 1
        c = self.cnt[key]
        for u in ru:
            self.readers.setdefault(u, {})[key] = c
        for u in wu:
            self.lastw[u] = (key, c)
            self.readers[u] = {}


def chunks(lo, hi, step):
    out = []
    a = lo
    while a < hi:
        out.append((a, min(hi, a + step)))
        a += step
    return out


def build_program():
    nc = bass.Bass("TRN2", target_bir_lowering=False)
    dn = set()

    def dram(name, shape, dt, kind):
        dn.add(name)
        return nc.dram_tensor(name, shape, dt, kind=kind).ap()

    xin = {'p': dram("xp", [LP + HALO, D], F32, "ExternalInput"),
           's': dram("xs", [LS + HALO, D], F32, "ExternalInput")}
    yout = {'p': dram("yp", [LP, D], F32, "ExternalOutput"),
            's': dram("ys", [LS, D], F32, "ExternalOutput")}
    w_in = dram("w_in", [2, D, PW], F32, "ExternalInput")
    w_out = dram("w_out", [2, D, D], F32, "ExternalInput")
    w_up = dram("w_up", [2, D, 2 * DFF], F32, "ExternalInput")
    w_down = dram("w_down", [2, DFF, D], F32, "ExternalInput")
    cst_d = dram("cst", [128, NCST], F32, "ExternalInput")
    ident_d = dram("ident", [128, 128], F32, "ExternalInput")
    sel_d = dram("sel", [2, 128], F32, "ExternalInput")
    oh_d = dram("oh", [33, 768], F32, "ExternalInput")
    relb_d = dram("relb", [33, 8], F32, "ExternalInput")
    sink_d = dram("sink", [2, 8], F32, "ExternalInput")
    w_in_b = dram("w_in_b", [2, D, PW], BF16, "Internal")
    w_out_b = dram("w_out_b", [2, D, D], BF16, "Internal")
    w_up_b = dram("w_up_b", [2, D, 2 * DFF], BF16, "Internal")
    w_down_b = dram("w_down_b", [2, DFF, D], BF16, "Internal")
    scr = dram("scr", [3, 8, 128, 256], F32, "Internal")

    big = nc.alloc_sbuf_tensor("big", [128, NTOT], F32).ap()
    psn = set()
    PS = []
    for i in range(8):
        PS.append(nc.alloc_psum_tensor("ps%d" % i, [128, 512], F32).ap())
        psn.add("ps%d" % i)
    D0, D1, ST, ST2, S0, S1b, PO, PDN = PS
    T = Trk(nc, dn, psn)

    top = [0]

    def A32(n):
        o = top[0]
        top[0] += (n + 127) // 128 * 128
        assert top[0] <= NTOT, top[0]
        return big[:, o:o + n]

    def A16(n):
        o = top[0]
        nf = (n + 1) // 2
        top[0] += (nf + 127) // 128 * 128
        assert top[0] <= NTOT, top[0]
        return big[:, o:o + nf].bitcast(BF16)[:, 0:n]

    X = A32(8 * WMAX).rearrange("p (c t) -> p c t", c=8)
    TAB = A32(3 * 8 * 128).rearrange("p (k h q) -> p k h q", k=3, h=8)
    CST = A32(NCST)
    IDENT = A32(128)
    G32 = A32(40)
    EPSC = A32(8)
    ESKF = A32(8)
    SELF = A32(128)
    ONESB = A16(128)
    ONESL = A16(128)
    ONESR = A16(128)
    SELB = A16(128)
    ESK = A16(2 * 4 * 128).rearrange("p (l h q) -> p l h q", l=2, h=4)
    XST = A32(3 * 1024).rearrange("p (s f) -> p s f", s=3)
    CB = A16(4 * WMAX).rearrange("p (c t) -> p c t", c=4)
    OV0 = top[0]

    def mm(out, lhsT, rhs, start, stop):
        T.issue('pe', lambda: nc.tensor.matmul(out, lhsT, rhs, start=start, stop=stop),
                reads=[lhsT, rhs], writes=[out])

    def tr(out, in_, ident):
        T.issue('pe', lambda: nc.tensor.transpose(out, in_, ident), reads=[in_, ident], writes=[out])

    def act(out, in_, func, bias=None, scale=1.0, e='act'):
        rd = [in_]
        kw = {}
        if bias is not None:
            kw['bias'] = bias
            rd.append(bias)
        if not isinstance(scale, float):
            rd.append(scale)
        T.issue('act', lambda: nc.scalar.activation(out=out, in_=in_, func=func, scale=scale, **kw),
                reads=rd, writes=[out])

    def V(e):
        return nc.vector if e == 'dve' else nc.gpsimd

    def tt(out, in0, in1, op, e='dve'):
        T.issue(e, lambda: V(e).tensor_tensor(out=out, in0=in0, in1=in1, op=op), reads=[in0, in1], writes=[out])

    def ts(out, in0, s1, s2, op0, op1=None, e='dve'):
        rd = [in0] + [s for s in (s1, s2) if s is not None and not isinstance(s, float)]
        if op1 is None:
            T.issue(e, lambda: V(e).tensor_scalar(out=out, in0=in0, scalar1=s1, scalar2=None, op0=op0),
                    reads=rd, writes=[out])
        else:
            T.issue(e, lambda: V(e).tensor_scalar(out=out, in0=in0, scalar1=s1, scalar2=s2, op0=op0, op1=op1),
                    reads=rd, writes=[out])

    def stt(out, in0, scalar, in1, op0, op1, e='dve'):
        rd = [in0, in1] + ([] if isinstance(scalar, float) else [scalar])
        T.issue(e, lambda: V(e).scalar_tensor_tensor(out=out, in0=in0, scalar=scalar, in1=in1, op0=op0, op1=op1),
                reads=rd, writes=[out])

    def cp(out, in_, e='dve'):
        T.issue(e, lambda: V(e).tensor_copy(out=out, in_=in_), reads=[in_], writes=[out])

    def mset(ap, val, e='dve'):
        T.issue(e, lambda: V(e).memset(ap, val), writes=[ap])

    def recip(out, in_):
        T.issue('dve', lambda: nc.vector.reciprocal(out=out, in_=in_), reads=[in_], writes=[out])

    def rsq(out, in_, eps_ap):
        act(out, in_, AF.Ln, bias=eps_ap)
        act(out, out, AF.Exp, bias=EPSC[:, 2:3], scale=-0.5)

    CASTS = {'win': (w_in, w_in_b, D, 256), 'wout': (w_out, w_out_b, D, 256),
             'wup': (w_up, w_up_b, D, 128), 'wdn': (w_down, w_down_b, DFF, 256)}

    def cast(nm, l, gate=None):
        src, dst, rows, step = CASTS[nm]
        pairs = [(dst[l, a:b, :], src[l, a:b, :]) for a, b in chunks(0, rows, step)]
        T.dma('pool', 'cast_%s%d' % (nm, l), pairs, wkeys=[('dr', nm, l)], rkeys=([gate] if gate else []))

    cast('win', 0)
    stage = {'B': False, 'C': False}

    T.dma('sp', 'k_cst', [(CST, cst_d)])
    T.dma('sp', 'k_ident', [(IDENT, ident_d)])
    T.dma('sp', 'k_sel', [(SELF[0:2, :], sel_d)])
    T.dma('sp', 'k_sink', [(ESKF[0:2, :], sink_d)])
    mset(ONESB, 1.0)
    mset(ONESL, 0.0)
    mset(ONESL[:, 0:64], 1.0)
    mset(ONESR, 0.0)
    mset(ONESR[:, 64:128], 1.0)
    mset(EPSC[:, 0:1], EPS * 1024.0)
    mset(EPSC[:, 1:2], EPS)
    mset(EPSC[:, 2:3], 0.0)
    cp(SELB[0:2, :], SELF[0:2, :])
    ts(G32, CST[:, 0:40], 32.0, None, ALU.mult)
    act(ESKF[0:2, :], ESKF[0:2, :], AF.Exp, bias=EPSC[0:2, 2:3])
    for l in range(2):
        cp(ESK[0:2, l, :, :], ESKF[0:2, l * 4:(l + 1) * 4].unsqueeze(2).to_broadcast([2, 4, 128]))

    ov = [OV0]

    def O32(n):
        o = ov[0]
        ov[0] += (n + 127) // 128 * 128
        assert ov[0] <= NTOT, ov[0]
        return big[:, o:o + n]

    def O16(n):
        o = ov[0]
        nf = (n + 1) // 2
        ov[0] += (nf + 127) // 128 * 128
        assert ov[0] <= NTOT, ov[0]
        return big[:, o:o + nf].bitcast(BF16)[:, 0:n]

    ov[0] = OV0
    OHs = O32(768)
    RBs = O32(8)
    Gs = O32(768)
    T.dma('sp', 'k_oh', [(OHs[0:33, :], oh_d)])
    T.dma('sp', 'k_rb', [(RBs[0:33, :], relb_d)])
    mm(D0[0:8, 0:384], RBs[0:33, 0:8], OHs[0:33, 0:384], True, True)
    mm(D1[0:8, 0:384], RBs[0:33, 0:8], OHs[0:33, 384:768], True, True)
    cp(Gs[0:8, 0:384], D0[0:8, 0:384])
    cp(Gs[0:8, 384:768], D1[0:8, 0:384])
    def scr_dma():
        T.dma('sp', 'k_scr', [(scr[kb], Gs[0:8, kb * 256:(kb + 1) * 256].unsqueeze(1).to_broadcast([8, 128, 256]))
                              for kb in range(3)], wkeys=[('dr', 'scr')])

    tab_done = [False]

    def tab_dma():
        if tab_done[0]:
            return
        tab_done[0] = True
        T.dma('sp', 'k_tab', [(TAB[:, kb, :, :], bass.AP(scr.tensor, kb * 8 * 32768 + 128, [[255, 128], [32768, 8], [1, 128]]))
                              for kb in range(3)], rkeys=[('dr', 'scr')])

    def rms_a(XSQ, a, b, bank=None):
        bank = ST if bank is None else bank
        n = b - a
        for c in range(8):
            act(XSQ[:, c % 2, 0:n], X[:, c, a:b], AF.Square)
            mm(bank[:, 0:n], ONESB, XSQ[:, c % 2, 0:n], c == 0, c == 7)

    def rms_b(H, RSTD, gcol, a, b, bank=None):
        bank = ST if bank is None else bank
        n = b - a
        rsq(RSTD[:, 0:n], bank[:, 0:n], EPSC[:, 0:1])
        for c in range(8):
            stt(H[:, c, 0:n], X[:, c, a:b], G32[:, gcol + c:gcol + c + 1], RSTD[:, 0:n], ALU.mult, ALU.mult)

    def rms(H, XSQ, RSTD, gcol, a, b):
        rms_a(XSQ, a, b)
        rms_b(H, RSTD, gcol, a, b)

    dsel = [0]
    DB6 = [D0, D1, S0, S1b, PO, PDN]
    DB2 = [D0, D1]
    dlist = [DB6]

    def dbank():
        dsel[0] += 1
        return dlist[0][dsel[0] % len(dlist[0])]

    ev = [0]

    NSLOT = 3
    xslot = [0]

    def in_prefetch(tile, a, b, s):
        which, g0, HL, W = tile
        n = b - a
        T.dma('sp', 'k_xst%d' % s, [(XST[0:n, s, :], xin[which][g0 + a:g0 + b, :])])

    def in_consume(tile, a, b, s):
        n = b - a
        for half in range(2):
            pb = dbank()
            for c4 in range(4):
                c = half * 4 + c4
                tr(pb[:, c4 * 128:c4 * 128 + n], XST[0:n, s, c * 128:(c + 1) * 128], IDENT[0:n, 0:n])
            src = pb.rearrange("p (c t) -> p c t", c=4)[:, :, 0:n]
            dst = X[:, half * 4:half * 4 + 4, a:b]
            cp(dst, src)

    def tile_body(tile):
        which, g0, HL, W = tile
        Pr = (0, W)
        R1 = (max(0, HL - 130), HL + OWN + 130)
        R2 = (max(0, HL - 1), HL + OWN + 1)
        for l, (P, R) in enumerate(((Pr, R1), (R1, R2))):
            layer(l, P, R)

    def tile_output(tile, nxt):
        which, g0, HL, W = tile
        ydst = yout[which]
        ov[0] = OV0
        XSQ = O16(2 * 512).rearrange("p (s t) -> p s t", s=2)
        RSTDS = [O32(512) for _ in range(2)]
        XNS = [O32(8 * 512).rearrange("p (c t) -> p c t", c=8) for _ in range(2)]
        blocks = chunks(0, nxt[3], 128) if nxt is not None else []
        st = {'pref': 0, 'cons': 0}

        def prefetch_upto(k):
            while st['pref'] < min(k, len(blocks)):
                j = st['pref']
                in_prefetch(nxt, blocks[j][0], blocks[j][1], j % 2)
                st['pref'] += 1

        def flush(pred):
            while st['cons'] < len(blocks) and pred(*blocks[st['cons']]):
                j = st['cons']
                prefetch_upto(j + 1)
                in_consume(nxt, blocks[j][0], blocks[j][1], j % 2)
                st['cons'] += 1
                prefetch_upto(j + 3)

        prefetch_upto(2)
        flush(lambda ia, ib: ib <= HL or ia >= HL + OWN)
        for gi2, (ga, gb) in enumerate(chunks(HL, HL + OWN, 512)):
            gn = gb - ga
            RSTD = RSTDS[gi2 % 2]
            XN = XNS[gi2 % 2]
            for c in range(8):
                act(XSQ[:, c % 2, 0:gn], X[:, c, ga:gb], AF.Square)
                mm(ST[:, 0:gn], ONESB, XSQ[:, c % 2, 0:gn], c == 0, c == 7)
            rsq(RSTD[:, 0:gn], ST[:, 0:gn], EPSC[:, 0:1])
            for c in range(8):
                stt(XN[:, c, 0:gn], X[:, c, ga:gb], G32[:, 32 + c:33 + c], RSTD[:, 0:gn], ALU.mult, ALU.mult)
            for bi, (a, b) in enumerate(chunks(ga, gb, 128)):
                n = b - a
                s = 2
                for half in range(2):
                    pb = dbank()
                    for c4 in range(4):
                        c = half * 4 + c4
                        tr(pb[0:n, c4 * 128:(c4 + 1) * 128], XN[:, c, a - ga:b - ga], IDENT)
                    act(XST[0:n, s, half * 512:(half + 1) * 512], pb[0:n, :], AF.Copy)
                T.dma('sp', 'k_xst%d' % s, [(ydst[g0 + a:g0 + b, :], XST[0:n, s, :])])
                flush(lambda ia, ib: ib <= ga)
        flush(lambda ia, ib: True)

    def layer(l, P, R):
        ov[0] = OV0
        DIAG = O16(4 * 31 * 128).rearrange("p (m j q) -> p m j q", m=4, j=31)
        WCV = O16(8 * 1024).rearrange("p (k n) -> p k n", k=8)
        HH = [O16(8 * 512).rearrange("p (c t) -> p c t", c=8)]
        XSQ = O16(2 * 512).rearrange("p (s t) -> p s t", s=2)
        RSTDS = [O32(512)]
        HH.append(O16(8 * 512).rearrange("p (c t) -> p c t", c=8))
        RSTDS.append(O32(512))
        GLU = O16(4 * WMAX).rearrange("p (c t) -> p c t", c=4)
        SIGS = [O32(512) for _ in range(2)]
        Y32s = O32(2 * 4 * 256).rearrange("p (s c t) -> p s c t", s=2, c=4)
        YBFs = O16(2 * 4 * 256).rearrange("p (s c t) -> p s c t", s=2, c=4)
        YSQs = O16(2 * 4 * 256).rearrange("p (s c t) -> p s c t", s=2, c=4)
        MEANs = O32(2 * 256).rearrange("p (s t) -> p s t", s=2)
        VARs = O32(2 * 256).rearrange("p (s t) -> p s t", s=2)
        RSCs = O32(2 * 256).rearrange("p (s t) -> p s t", s=2)
        TTs = O32(2 * 2 * 256).rearrange("p (s k t) -> p s k t", s=2, k=2)
        T.dma('sp', 'k_wcv', [(WCV, w_in_b[l].rearrange("(k p) n -> p k n", p=128)[:, :, 768:1792])],
              rkeys=[('dr', 'win', l)], wkeys=([] if stage['B'] else [('gate', 'B')]))
        if not stage['B']:
            stage['B'] = True
            cast('wout', 0, gate=('gate', 'B'))
            cast('wup', 0, gate=('gate', 'B'))
            cast('wdn', 0, gate=('gate', 'B'))
        def build_diag(m):
            col = C_CW + (l * 4 + m) * 31
            T.issue('dve', lambda: nc.vector.tensor_tensor(
                out=DIAG[:, m, :, :], in0=IDENT.unsqueeze(1).to_broadcast([128, 31, 128]),
                in1=CST[:, col:col + 31].unsqueeze(2).to_broadcast([128, 31, 128]), op=ALU.mult),
                reads=[IDENT, CST[:, col:col + 31]], writes=[DIAG[:, m, :, :]])
        Gr = (max(P[0], R[0] - 15), min(P[1], R[1] + 15))
        gch = chunks(Gr[0], Gr[1], 512)
        rms(HH[0], XSQ, RSTDS[0], C_GA + l * 8, gch[0][0], gch[0][1])
        for ci, (a, b) in enumerate(gch):
            n = b - a
            H = HH[ci % 2]
            if ci + 1 < len(gch):
                rms(HH[(ci + 1) % 2], XSQ, RSTDS[(ci + 1) % 2], C_GA + l * 8, gch[ci + 1][0], gch[ci + 1][1])
            for m in range(4):
                SIG = SIGS[m % 2]
                pb = dbank()
                for k in range(8):
                    mm(pb[:, 0:n], WCV[:, k, 512 + m * 128:512 + (m + 1) * 128], H[:, k, 0:n], k == 0, k == 7)
                act(SIG[:, 0:n], pb[:, 0:n], AF.Sigmoid)
                pa = dbank()
                for k in range(8):
                    mm(pa[:, 0:n], WCV[:, k, m * 128:(m + 1) * 128], H[:, k, 0:n], k == 0, k == 7)
                tt(GLU[:, m, a:b], pa[:, 0:n], SIG[:, 0:n], ALU.mult)
                if ci == 0:
                    build_diag(m)
        cch = chunks(R[0], R[1], 256)

        def cbufs(cci):
            cs = cci % 2
            return (Y32s[:, cs], YBFs[:, cs], YSQs[:, cs], MEANs[:, cs], VARs[:, cs], RSCs[:, cs], TTs[:, cs])

        def conv_stat(cci, m):
            a, b = cch[cci]
            n = b - a
            Y32, YBF, YSQ = cbufs(cci)[0:3]
            mm(ST[:, 0:n], ONESB, YBF[:, m, 0:n], m == 0, m == 3)
            mm(ST2[:, 0:n], ONESB, YSQ[:, m, 0:n], m == 0, m == 3)

        def conv_m(cci, m):
            a, b = cch[cci]
            n = b - a
            Y32, YBF, YSQ = cbufs(cci)[0:3]
            pb = dbank()
            taps = []
            for j in [15] + [j for j in range(31) if j != 15]:
                sh = j - 15
                oa = max(a, Gr[0] - sh)
                ob = min(b, Gr[1] - sh)
                if ob > oa:
                    taps.append((j, sh, oa, ob))
            for ti, (j, sh, oa, ob) in enumerate(taps):
                mm(pb[:, oa - a:ob - a], DIAG[:, m, j, :], GLU[:, m, oa + sh:ob + sh], ti == 0, ti == len(taps) - 1)
            act(Y32[:, m, 0:n], pb[:, 0:n], AF.Identity, bias=CST[:, C_CB + l * 4 + m:C_CB + l * 4 + m + 1])
            cp(YBF[:, m, 0:n], Y32[:, m, 0:n])
            act(YSQ[:, m, 0:n], Y32[:, m, 0:n], AF.Square)

        def ln_tail(cci):
            a, b = cch[cci]
            n = b - a
            Y32, YBF, YSQ, MEAN, VAR, RSC, TT = cbufs(cci)
            act(MEAN[:, 0:n], ST[:, 0:n], AF.Copy, scale=1.0 / 512.0)
            tt(VAR[:, 0:n], MEAN[:, 0:n], MEAN[:, 0:n], ALU.mult)
            stt(VAR[:, 0:n], ST2[:, 0:n], 1.0 / 512.0, VAR[:, 0:n], ALU.mult, ALU.subtract)
            rsq(RSC[:, 0:n], VAR[:, 0:n], EPSC[:, 1:2])
            for m in range(4):
                tt(TT[:, m % 2, 0:n], Y32[:, m, 0:n], MEAN[:, 0:n], ALU.subtract)
                tt(TT[:, m % 2, 0:n], TT[:, m % 2, 0:n], RSC[:, 0:n], ALU.mult)
                act(CB[:, m, a:b], TT[:, m % 2, 0:n], AF.Silu,
                    bias=CST[:, C_LB + l * 4 + m:C_LB + l * 4 + m + 1],
                    scale=CST[:, C_LG + l * 4 + m:C_LG + l * 4 + m + 1])

        for m in range(4):
            conv_m(0, m)
            if m >= 1:
                conv_stat(0, m - 1)
        for cci in range(len(cch)):
            if cci + 1 < len(cch):
                conv_m(cci + 1, 0)
            conv_stat(cci, 3)
            ln_tail(cci)
            if cci + 1 < len(cch):
                for m in range(1, 4):
                    conv_m(cci + 1, m)
                    conv_stat(cci + 1, m - 1)

        ov[0] = OV0
        HH = [O16(8 * 512).rearrange("p (c t) -> p c t", c=8) for _ in range(2)]
        XSQ = O16(2 * 512).rearrange("p (s t) -> p s t", s=2)
        RSTDS = [O32(512) for _ in range(2)]
        WQ = O16(8 * 768).rearrange("p (k n) -> p k n", k=8)
        WO = O16(8 * 1024).rearrange("p (k n) -> p k n", k=8)
        Q = O16(4 * WMAX).rearrange("p (c t) -> p c t", c=4)
        KT = O16(WMAX)
        NB = (WMAX + 127) // 128
        VP = O16(NB * 2 * 128).rearrange("p (b g d) -> p b g d", b=NB, g=2)
        SSB = O32(4 * 384).rearrange("p (s k q) -> p s k q", s=4, k=3)
        PT = O16(2 * 3 * 8 * 128).rearrange("p (s k h q) -> p s k h q", s=2, k=3, h=8)
        REC = O32(512)
        T.dma('sp', 'k_wq', [(WQ, w_in_b[l].rearrange("(k p) n -> p k n", p=128)[:, :, 0:768])],
              rkeys=[('dr', 'win', l)])
        T.dma('sp', 'k_wo', [(WO, w_out_b[l].rearrange("(k p) n -> p k n", p=128))],
              rkeys=[('dr', 'wout', l)], wkeys=([] if stage['C'] else [('gate', 'C')]))
        tab_dma()
        if not stage['C']:
            stage['C'] = True
            for nm in ('win', 'wout', 'wup', 'wdn'):
                cast(nm, 1, gate=('gate', 'C'))
        mset(VP, 0.0, e='pool')
        Kr = (max(P[0], R[0] - 128) // 128 * 128, min(P[1], R[1] + 128))
        pchunks = chunks(Kr[0], Kr[1], 512)

        def proj_items(ci):
            a, b = pchunks[ci]
            n = b - a
            H = HH[ci % 2]
            items = []

            def qitem(m):
                pb = dbank()
                for k in range(8):
                    mm(pb[:, 0:n], WQ[:, k, m * 128:(m + 1) * 128], H[:, k, 0:n], k == 0, k == 7)
                act(Q[:, m, a:b], pb[:, 0:n], AF.Copy, scale=0.125)

            def kitem():
                pb = dbank()
                for k in range(8):
                    mm(pb[:, 0:n], WQ[:, k, 512:640], H[:, k, 0:n], k == 0, k == 7)
                act(KT[:, a:b], pb[:, 0:n], AF.Copy)

            def vitem(ba, bb):
                nb = bb - ba
                blk = ba // 128
                pb = dbank()
                for k in range(8):
                    mm(pb[0:nb, 0:128], H[:, k, ba - a:bb - a], WQ[:, k, 640:768], k == 0, k == 7)
                act(VP[0:nb, blk, 0, 0:64], pb[0:nb, 0:64], AF.Copy)
                act(VP[0:nb, blk, 1, 64:128], pb[0:nb, 64:128], AF.Copy)

            if b > R[0] and a < R[1]:
                for m in range(4):
                    items.append(lambda m=m: qitem(m))
            items.append(kitem)
            for (ba, bb) in chunks(a, b, 128):
                items.append(lambda ba=ba, bb=bb: vitem(ba, bb))
            return items

        def wout_items(a, b):
            n = b - a

            def witem(m):
                pb = dbank()
                for k in range(8):
                    rhs = Q[:, k, a:b] if k < 4 else CB[:, k - 4, a:b]
                    mm(pb[:, 0:n], WO[:, k, m * 128:(m + 1) * 128], rhs, k == 0, k == 7)
                tt(X[:, m, a:b], pb[:, 0:n], X[:, m, a:b], ALU.add)
            return [lambda m=m: witem(m) for m in range(8)]

        dlist[0] = [D0, ST2]
        qtiles = []
        for nblk in range(R[0] // 128, (R[1] - 1) // 128 + 1):
            qa = max(R[0], nblk * 128)
            qb = min(R[1], nblk * 128 + 128)
            qtiles.append((nblk, qa, qb))
        sidx = [0]
        sbi = [0]
        SB6 = [S0, S1b, D1]
        POs = [PO, PO]
        PDs = [PDN, PDN]
        def att_S(ti):
            nblk, qa, qb = qtiles[ti]
            nq = qb - qa
            qo = qa - nblk * 128
            kbs = []
            for kb in range(3):
                kblk = nblk - 1 + kb
                k0 = kblk * 128
                if k0 < Kr[0] or k0 >= Kr[1]:
                    continue
                kbs.append((kb, kblk, min(128, Kr[1] - k0)))
            sp = sidx[0] % 2
            sidx[0] += 1
            for h4 in range(4):
                pss = []
                for g in range(2):
                    psb = SB6[sbi[0] % 3]
                    sbi[0] += 1
                    pss.append(psb[:, 0:384].rearrange("p (k q) -> p k q", k=3))
                for (kb, kblk, nk) in kbs:
                    for g in range(2):
                        mm(pss[g][0:nk, kb, 0:nq], KT[g * 64:(g + 1) * 64, kblk * 128:kblk * 128 + nk],
                           Q[g * 64:(g + 1) * 64, h4, qa:qb], True, True)
                sb0 = 2 * (h4 % 2)
                if all(nk == 128 for (_, _, nk) in kbs):
                    k0, k1 = kbs[0][0], kbs[-1][0] + 1
                    for g in range(2):
                        tt(SSB[:, sb0 + g, k0:k1, 0:nq], pss[g][:, k0:k1, 0:nq],
                           TAB[:, k0:k1, g * 4 + h4, qo:qo + nq], ALU.add)
                    act(PT[:, sp, k0:k1, h4::4, 0:nq].rearrange("p k g q -> p g k q"),
                        SSB[:, sb0:sb0 + 2, k0:k1, 0:nq], AF.Exp, bias=EPSC[:, 2:3])
                else:
                    for g in range(2):
                        h = g * 4 + h4
                        ps3 = pss[g]
                        sb = sb0 + g
                        for (kb, kblk, nk) in kbs:
                            tt(SSB[0:nk, sb, kb, 0:nq], ps3[0:nk, kb, 0:nq], TAB[0:nk, kb, h, qo:qo + nq], ALU.add)
                            act(PT[0:nk, sp, kb, h, 0:nq], SSB[0:nk, sb, kb, 0:nq], AF.Exp, bias=EPSC[0:nk, 2:3])
            return (nq, qo, kbs, sp, qa, qb)

        def att_P(st):
            nq, qo, kbs, sp, qa, qb = st
            po3 = POs[sp].rearrange("p (h q) -> p h q", h=4)[:, :, 0:nq]
            pd3 = PDs[sp].rearrange("p (h q) -> p h q", h=4)[:, :, 0:nq]
            seq = [(g, kbt) for g in range(2) for kbt in kbs]
            for i, (g, (kb, kblk, nk)) in enumerate(seq):
                mm(po3, VP[0:nk, kblk, g, :], PT[0:nk, sp, kb, g * 4:(g + 1) * 4, 0:nq], i == 0, i == len(seq) - 1)
            for i, (g, (kb, kblk, nk)) in enumerate(seq):
                mm(pd3, (ONESL if g == 0 else ONESR)[0:nk, :], PT[0:nk, sp, kb, g * 4:(g + 1) * 4, 0:nq], i == 0, False)
            mm(pd3, SELB[0:2, :], ESK[0:2, l, :, 0:nq], False, True)
            rec3 = REC.rearrange("p (h q) -> p h q", h=4)[:, :, 0:nq]
            act(rec3, pd3, AF.Ln)
            act(rec3, rec3, AF.Exp, bias=EPSC[:, 2:3], scale=-1.0)
            tt(Q[:, :, qa:qb], po3, rec3, ALU.mult)
        fill = []
        proj_done = [0]

        def rms_item(ci):
            a, b = pchunks[ci]
            return lambda: rms(HH[ci % 2], XSQ, RSTDS[ci % 2], C_GA + l * 8, a, b)

        def queue_proj_ahead():
            if proj_done[0] < len(pchunks):
                ci = proj_done[0]
                if ci == 0:
                    fill.append(('p', ci, rms_item(0)))
                if ci + 1 < len(pchunks):
                    fill.append(('p', ci, rms_item(ci + 1)))
                for it in proj_items(ci):
                    fill.append(('p', ci, it))
                proj_done[0] += 1

        def need_proj(upto_tok):
            while proj_done[0] < len(pchunks) and pchunks[proj_done[0]][0] < upto_tok:
                queue_proj_ahead()
            cmax = max([ci for ci in range(len(pchunks)) if pchunks[ci][0] < upto_tok] + [-1])
            while any(f[0] == 'p' and f[1] <= cmax for f in fill):
                fill.pop(0)[2]()

        def run_fill(k):
            for _ in range(k):
                if fill:
                    fill.pop(0)[2]()

        wch = chunks(R[0], R[1], 512)
        wdone = [0]
        need_proj(min(Kr[1], qtiles[0][0] * 128 + 256))
        stS = att_S(0)
        for ti in range(len(qtiles)):
            if ti + 1 < len(qtiles):
                need_proj(min(Kr[1], qtiles[ti + 1][0] * 128 + 256))
                nxt = att_S(ti + 1)
            else:
                nxt = None
            if not fill:
                queue_proj_ahead()
            run_fill(4)
            att_P(stS)
            while wdone[0] < len(wch) and wch[wdone[0]][1] <= qtiles[ti][2]:
                for it in wout_items(*wch[wdone[0]]):
                    fill.append(('w', -1, it))
                wdone[0] += 1
            run_fill(3)
            stS = nxt
        while proj_done[0] < len(pchunks):
            queue_proj_ahead()
        while wdone[0] < len(wch):
            for it in wout_items(*wch[wdone[0]]):
                fill.append(('w', -1, it))
            wdone[0] += 1
        run_fill(len(fill))

        ov[0] = OV0
        WU = []
        WD = []
        for _ in range(2):
            WU.append(O16(8 * 2 * GMAX * 128).rearrange("p (k n) -> p k n", k=8))
            WD.append(O16(GMAX * 1024).rearrange("p (j n) -> p j n", j=GMAX))
        HFW = OWN + HALO + 130
        HF = O16(8 * HFW).rearrange("p (c t) -> p c t", c=8)
        XSQ = O16(2 * 512).rearrange("p (s t) -> p s t", s=2)
        RSTD = O32(512)
        U1 = O32(3 * 512).rearrange("p (s t) -> p s t", s=3)
        U2 = O32(3 * 512).rearrange("p (s t) -> p s t", s=3)
        SS = O32(2 * 512).rearrange("p (s t) -> p s t", s=2)
        GB = O16(2 * GMAX * 512).rearrange("p (s j t) -> p s j t", s=2, j=GMAX)
        fpre = chunks(R[0], R[1], 512)
        fbanks = [ST, ST2, D0, D1]
        RSTDF = [RSTD, U1[:, 0, :], U1[:, 1, :], U1[:, 2, :]]
        for i, (a, b) in enumerate(fpre):
            rms_a(XSQ, a, b, bank=fbanks[i % 4])
        for i, (a, b) in enumerate(fpre):
            rms_b(HF[:, :, a:b], RSTDF[i % 4], C_GF + l * 8, a, b, bank=fbanks[i % 4])
        fch = chunks(R[0], R[1], 510)
        wupv = w_up_b[l].rearrange("(k p) n -> p k n", p=128)
        wdnv = w_down_b[l].rearrange("(j p) n -> p j n", p=128)
        DB8 = [D0, D1, S0, S1b, PO, PDN, ST, ST2]
        dlist[0] = DB8

        def ff_load(gi):
            j0, G = FF_GROUPS[gi]
            s = gi % 2
            T.dma('sp', 'k_ff%d' % s,
                  [(WU[s][:, :, 0:G * 128], wupv[:, :, j0 * 128:(j0 + G) * 128]),
                   (WU[s][:, :, GMAX * 128:GMAX * 128 + G * 128], wupv[:, :, (NFF + j0) * 128:(NFF + j0 + G) * 128]),
                   (WD[s][:, 0:G, :], wdnv[:, j0:j0 + G, :])],
                  rkeys=[('dr', 'wup', l), ('dr', 'wdn', l)])

        def ff_up(gi, ci, si):
            j0, G = FF_GROUPS[gi]
            s = gi % 2
            a, b = fch[ci]
            n = b - a
            ua = max(R[0], a - 1)
            ub = min(R[1], b + 1)
            nu = ub - ua
            gs = si % 2
            for jj in range(G):
                j = j0 + jj
                us = ev[0] % 3
                ev[0] += 1
                for half, (UU, colbase) in enumerate(((U1, jj * 128), (U2, GMAX * 128 + jj * 128))):
                    jc = j if half == 0 else NFF + j
                    pb = dbank()
                    for k in range(8):
                        mm(pb[:, 0:nu], WU[s][:, k, colbase:colbase + 128], HF[:, k, ua:ub], k == 0, k == 7)
                    wc = C_FW + (l * 44 + jc) * 3
                    act(UU[:, us, 0:n], pb[:, a - ua:a - ua + n], AF.Identity,
                        bias=CST[:, C_FB + l * 44 + jc:C_FB + l * 44 + jc + 1], scale=CST[:, wc + 1:wc + 2])
                    lo = max(a, ua + 1) - a
                    if n > lo:
                        stt(UU[:, us, lo:n], pb[:, a + lo - 1 - ua:a + n - 1 - ua], CST[:, wc:wc + 1],
                            UU[:, us, lo:n], ALU.mult, ALU.add)
                    hi = min(b, ub - 1) - a
                    if hi > 0:
                        stt(UU[:, us, 0:hi], pb[:, a + 1 - ua:a + hi + 1 - ua], CST[:, wc + 2:wc + 3],
                            UU[:, us, 0:hi], ALU.mult, ALU.add)
                act(SS[:, us % 2, 0:n], U1[:, us, 0:n], AF.Silu)
                tt(GB[:, gs, jj, 0:n], SS[:, us % 2, 0:n], U2[:, us, 0:n], ALU.mult, e='pool')

        def ff_down(gi, ci, si):
            j0, G = FF_GROUPS[gi]
            s = gi % 2
            a, b = fch[ci]
            n = b - a
            gs = si % 2
            for m in range(8):
                pb = dbank()
                for jj in range(G):
                    mm(pb[:, 0:n], WD[s][:, jj, m * 128:(m + 1) * 128], GB[:, gs, jj, 0:n], jj == 0, jj == G - 1)
                tt(X[:, m, a:b], pb[:, 0:n], X[:, m, a:b], ALU.add)

        steps = [(gi, ci) for gi in range(len(FF_GROUPS)) for ci in range(len(fch))]
        prev = None
        for si, (gi, ci) in enumerate(steps):
            if ci == 0:
                ff_load(gi)
            ff_up(gi, ci, si)
            if prev is not None:
                ff_down(*prev)
            prev = (gi, ci, si)
        ff_down(*prev)
        dlist[0] = DB6

    tiles = []
    for which, L in (('p', LP), ('s', LS)):
        for t in range(L // OWN):
            HL = 0 if t == 0 else HALO
            tiles.append((which, t * OWN - HL, HL, HL + OWN + HALO))
    b0 = chunks(0, tiles[0][3], 128)
    for j in range(min(2, len(b0))):
        in_prefetch(tiles[0], b0[j][0], b0[j][1], j % 2)
    for j in range(len(b0)):
        in_consume(tiles[0], b0[j][0], b0[j][1], j % 2)
        if j + 2 < len(b0):
            in_prefetch(tiles[0], b0[j + 2][0], b0[j + 2][1], j % 2)
    scr_dma()
    for ti, tile in enumerate(tiles):
        tile_body(tile)
        tile_output(tile, tiles[ti + 1] if ti + 1 < len(tiles) else None)
    for s in range(NSLOT):
        k = 'k_xst%d' % s
        nc.sync.wait_ge(T.sem[k], T.cnt[k])
    return nc, T


_CACHE = {}


def _perm_q():
    idx = []
    for c in range(4):
        idx += list(range(c * 64, c * 64 + 64)) + list(range((c + 4) * 64, (c + 4) * 64 + 64))
    return np.array(idx)


def _buckets(rel):
    nb = 16
    ret = (rel > 0).astype(np.int64) * nb
    n = np.abs(rel)
    max_exact = 8
    nf = np.maximum(n, 1).astype(np.float32)
    large = max_exact + (np.log(nf / np.float32(max_exact)) / np.float32(np.log(128 / max_exact))
                         * np.float32(nb - max_exact)).astype(np.int32)
    large = np.minimum(large, nb - 1)
    return ret + np.where(n < max_exact, n, large)


def _onehot(mirror):
    oh = np.zeros((33, 768), np.float32)
    for kb in range(3):
        m = np.arange(256)
        rel = kb * 128 - m
        if mirror:
            relb = -rel
        else:
            relb = rel
        b = _buckets(relb)
        valid = np.abs(rel) <= 128
        row = np.where(valid, b, 32)
        oh[row, kb * 256 + m] = 1.0
    return oh


def kernel(x_prompt, x_sample, rel_bias, norm_attn_g, w_in, attn_sink, conv_dw_w, conv_dw_b,
           conv_ln_g, conv_ln_b, w_out, norm_ffn_g, w_up, ffn_dw_w, ffn_dw_b, w_down, norm_final_g):
    f = lambda a: np.ascontiguousarray(np.asarray(a, dtype=np.float32))
    x_prompt, x_sample = f(x_prompt), f(x_sample)
    if 'nc' not in _CACHE:
        _CACHE['nc'] = build_program()
    nc, _ = _CACHE['nc']
    pq = _perm_q()
    w_in_p = f(w_in).copy()
    w_in_p[:, :, 0:512] = f(w_in)[:, :, pq]
    w_out_p = f(w_out).copy()
    w_out_p[:, 0:512, :] = f(w_out)[:, pq, :]
    w_up_f, w_down_f = f(w_up), f(w_down)

    def feat(v, nchunk):
        v = f(v)
        lead = v.shape[:-1]
        v = v.reshape(lead + (nchunk, 128))
        return np.moveaxis(v, -1, 0)

    def make_cst(mirror):
        cst = np.zeros((128, NCST), np.float32)
        cst[:, C_GA:C_GA + 16] = feat(norm_attn_g, 8).reshape(128, 16)
        cst[:, C_GF:C_GF + 16] = feat(norm_ffn_g, 8).reshape(128, 16)
        cst[:, C_GN:C_GN + 8] = feat(norm_final_g, 8).reshape(128, 8)
        cst[:, C_CB:C_CB + 8] = feat(conv_dw_b, 4).reshape(128, 8)
        cst[:, C_LG:C_LG + 8] = feat(conv_ln_g, 4).reshape(128, 8)
        cst[:, C_LB:C_LB + 8] = feat(conv_ln_b, 4).reshape(128, 8)
        cw = f(conv_dw_w)
        fw = f(ffn_dw_w)
        if mirror:
            cw = cw[:, ::-1, :]
            fw = fw[:, ::-1, :]
        cwt = feat(cw, 4)
        cst[:, C_CW:C_CW + 248] = np.transpose(cwt, (0, 1, 3, 2)).reshape(128, 248)
        fwt = feat(fw, 44)
        cst[:, C_FW:C_FW + 264] = np.transpose(fwt, (0, 1, 3, 2)).reshape(128, 264)
        cst[:, C_FB:C_FB + 88] = feat(ffn_dw_b, 44).reshape(128, 88)
        return np.ascontiguousarray(cst)

    ident = np.eye(128, dtype=np.float32)
    sel = np.zeros((2, 128), np.float32)
    sel[0, 0:64] = 1.0
    sel[1, 64:128] = 1.0
    relb = np.concatenate([f(rel_bias), np.full((1, 8), NEG, np.float32)], axis=0)
    sk = f(attn_sink)
    sink = np.ascontiguousarray(np.transpose(sk.reshape(2, 2, 4), (1, 0, 2)).reshape(2, 8))
    csts = [make_cst(False), make_cst(True)]
    ohs = [_onehot(False), _onehot(True)]
    in_maps = []
    for c in range(8):
        b, side = c // 2, c % 2
        xp = x_prompt[b]
        xs = x_sample[b]
        if side:
            xp = xp[::-1]
            xs = xs[::-1]
        in_maps.append({
            "xp": np.ascontiguousarray(xp[0:LP + HALO]), "xs": np.ascontiguousarray(xs[0:LS + HALO]),
            "w_in": w_in_p, "w_out": w_out_p, "w_up": w_up_f, "w_down": w_down_f,
            "cst": csts[side], "ident": ident, "sel": sel, "oh": ohs[side], "relb": relb, "sink": sink,
        })
    res = run_bass_kernel_spmd(nc, in_maps, core_ids=list(range(8)))
    yp = np.empty((4, 4096, D), np.float32)
    ys = np.empty((4, 8192, D), np.float32)
    for c in range(8):
        b, side = c // 2, c % 2
        rp = res.results[c]["yp"]
        rs = res.results[c]["ys"]
        if side:
            yp[b, 2048:] = rp[::-1]
            ys[b, 4096:] = rs[::-1]
        else:
            yp[b, :2048] = rp
            ys[b, :4096] = rs
    return yp, ys
```

```python
import numpy as np
import concourse.bass as bass
import concourse.mybir as mybir
from concourse.bass_utils import run_bass_kernel_spmd

F32 = mybir.dt.float32
BF16 = mybir.dt.bfloat16
AF = mybir.ActivationFunctionType
ALU = mybir.AluOpType

D = 1024
DFF = 2816
NFF = 22
PW = 1792
OWN = 1024
HALO = 258
WMAX = OWN + 2 * HALO
EPS = 1e-6
NEG = -30000.0
LP = 2048
LS = 4096
NTOT = 53200
FF_GROUPS = [(0, 5), (5, 5), (10, 4), (14, 4), (18, 4)]
GMAX = 5

C_GA = 0
C_GF = 16
C_GN = 32
C_CB = 40
C_LG = 48
C_LB = 56
C_CW = 64
C_FW = 312
C_FB = 576
NCST = 664


class Trk:
    def __init__(self, nc, dram_names, psum_names):
        self.nc = nc
        self.eng = {'pe': nc.tensor, 'act': nc.scalar, 'dve': nc.vector, 'pool': nc.gpsimd, 'sp': nc.sync}
        self.sem = {}
        self.cnt = {}
        for e in ['pe', 'act', 'dve', 'pool']:
            self.sem[e] = nc.alloc_semaphore('s_' + e)
            self.cnt[e] = 0
        self.waited = {}
        self.lastw = {}
        self.readers = {}
        self.dram_names = dram_names
        self.psum_names = psum_names
        self.ninst = 0

    def units(self, ap):
        name = ap.tensor.name
        if name in self.psum_names:
            return [('ps', name)]
        if name in self.dram_names:
            return []
        es = 4 if ap.dtype == F32 else 2
        col = (ap.offset * es) % (NTOT * 4)
        ext = 1
        for st, cn in ap.ap[1:]:
            ext += (cn - 1) * abs(st)
        lo = col
        hi = col + ext * es
        return [(name, u) for u in range(lo // 512, (hi - 1) // 512 + 1)]

    def _need(self, e, ru, wu):
        need = {}
        for u in ru:
            lw = self.lastw.get(u)
            if lw:
                need[lw[0]] = max(need.get(lw[0], 0), lw[1])
        for u in wu:
            lw = self.lastw.get(u)
            if lw and lw[0] != e:
                need[lw[0]] = max(need.get(lw[0], 0), lw[1])
            for p, c in self.readers.get(u, {}).items():
                if p != e:
                    need[p] = max(need.get(p, 0), c)
        for p, c in need.items():
            if self.waited.get((e, p), 0) < c:
                self.eng[e].wait_ge(self.sem[p], c)
                self.waited[(e, p)] = c

    def issue(self, e, fn, reads=(), writes=(), rkeys=(), wkeys=()):
        ru = [u for a in reads for u in self.units(a)] + list(rkeys)
        wu = [u for a in writes for u in self.units(a)] + list(wkeys)
        self._need(e, ru, wu)
        inst = fn()
        self.cnt[e] += 1
        c = self.cnt[e]
        inst.then_inc(self.sem[e], 1)
        self.ninst += 1
        for u in ru:
            self.readers.setdefault(u, {})[e] = c
        for u in wu:
            self.lastw[u] = (e, c)
            self.readers[u] = {}

    def dma(self, q, key, pairs, rkeys=(), wkeys=()):
        if key not in self.sem:
            self.sem[key] = self.nc.alloc_semaphore('d_' + str(len(self.sem)))
            self.cnt[key] = 0
        ru = list(rkeys)
        wu = list(wkeys)
        for o, i in pairs:
            ru += self.units(i)
            wu += self.units(o)
        self._need(q, ru, wu)
        for o, i in pairs:
            self.eng[q].dma_start(out=o, in_=i).then_inc(self.sem[key], 16)
            self.cnt[key] += 16
            self.ninst += 1
        c = self.cnt[key]
        for u in ru:
            self.readers.setdefault(u, {})[key] = c
        for u in wu:
            self.lastw[u] = (key, c)
            self.readers[u] = {}


def chunks(lo, hi, step):
    out = []
    a = lo
    while a < hi:
        out.append((a, min(hi, a + step)))
        a += step
    return out


def bchunks(lo, hi, maxstep, mult=2):
    n = hi - lo
    k = (n + maxstep - 1) // maxstep
    step = (n + k - 1) // k
    step = min(maxstep, (step + mult - 1) // mult * mult)
    return chunks(lo, hi, step)


def build_program():
    nc = bass.Bass("TRN2", target_bir_lowering=False)
    dn = set()

    def dram(name, shape, dt, kind):
        dn.add(name)
        return nc.dram_tensor(name, shape, dt, kind=kind).ap()

    xin = {'p': dram("xp", [LP + HALO, D], F32, "ExternalInput"),
           's': dram("xs", [LS + HALO, D], F32, "ExternalInput")}
    yout = {'p': dram("yp", [LP, D], F32, "ExternalOutput"),
            's': dram("ys", [LS, D], F32, "ExternalOutput")}
    w_in = dram("w_in", [2, D, PW], F32, "ExternalInput")
    w_out = dram("w_out", [2, D, D], F32, "ExternalInput")
    w_up = dram("w_up", [2, D, 2 * DFF], F32, "ExternalInput")
    w_down = dram("w_down", [2, DFF, D], F32, "ExternalInput")
    cst_d = dram("cst", [128, NCST], F32, "ExternalInput")
    ident_d = dram("ident", [128, 128], F32, "ExternalInput")
    sel_d = dram("sel", [2, 128], F32, "ExternalInput")
    oh_d = dram("oh", [33, 768], F32, "ExternalInput")
    relb_d = dram("relb", [33, 8], F32, "ExternalInput")
    sink_d = dram("sink", [2, 8], F32, "ExternalInput")
    w_in_b = dram("w_in_b", [2, D, PW], BF16, "Internal")
    w_out_b = dram("w_out_b", [2, D, D], BF16, "Internal")
    w_up_b = dram("w_up_b", [2, D, 2 * DFF], BF16, "Internal")
    w_down_b = dram("w_down_b", [2, DFF, D], BF16, "Internal")
    scr = dram("scr", [3, 8, 128, 256], F32, "Internal")

    big = nc.alloc_sbuf_tensor("big", [128, NTOT], F32).ap()
    psn = set()
    PS = []
    for i in range(8):
        PS.append(nc.alloc_psum_tensor("ps%d" % i, [128, 512], F32).ap())
        psn.add("ps%d" % i)
    D0, D1, ST, ST2, S0, S1b, PO, PDN = PS
    T = Trk(nc, dn, psn)

    top = [0]

    def A32(n):
        o = top[0]
        top[0] += (n + 127) // 128 * 128
        assert top[0] <= NTOT, top[0]
        return big[:, o:o + n]

    def A16(n):
        o = top[0]
        nf = (n + 1) // 2
        top[0] += (nf + 127) // 128 * 128
        assert top[0] <= NTOT, top[0]
        return big[:, o:o + nf].bitcast(BF16)[:, 0:n]

    X = A32(8 * WMAX).rearrange("p (c t) -> p c t", c=8)
    TAB = A32(3 * 8 * 128).rearrange("p (k h q) -> p k h q", k=3, h=8)
    CST = A32(NCST)
    IDENT = A32(128)
    G32 = A32(40)
    EPSC = A32(8)
    ESKF = A32(8)
    SELF = A32(128)
    ONESB = A16(128)
    ONESL = A16(128)
    ONESR = A16(128)
    SELB = A16(128)
    ESK = A16(2 * 4 * 128).rearrange("p (l h q) -> p l h q", l=2, h=4)
    XST = A32(3 * 1024).rearrange("p (s f) -> p s f", s=3)
    CB = A16(4 * WMAX).rearrange("p (c t) -> p c t", c=4)
    OV0 = top[0]

    def mm(out, lhsT, rhs, start, stop):
        T.issue('pe', lambda: nc.tensor.matmul(out, lhsT, rhs, start=start, stop=stop),
                reads=[lhsT, rhs], writes=[out])

    def tr(out, in_, ident):
        T.issue('pe', lambda: nc.tensor.transpose(out, in_, ident), reads=[in_, ident], writes=[out])

    def act(out, in_, func, bias=None, scale=1.0, e='act'):
        rd = [in_]
        kw = {}
        if bias is not None:
            kw['bias'] = bias
            rd.append(bias)
        if not isinstance(scale, float):
            rd.append(scale)
        T.issue('act', lambda: nc.scalar.activation(out=out, in_=in_, func=func, scale=scale, **kw),
                reads=rd, writes=[out])

    def V(e):
        return nc.vector if e == 'dve' else nc.gpsimd

    def tt(out, in0, in1, op, e='dve'):
        T.issue(e, lambda: V(e).tensor_tensor(out=out, in0=in0, in1=in1, op=op), reads=[in0, in1], writes=[out])

    def ts(out, in0, s1, s2, op0, op1=None, e='dve'):
        rd = [in0] + [s for s in (s1, s2) if s is not None and not isinstance(s, float)]
        if op1 is None:
            T.issue(e, lambda: V(e).tensor_scalar(out=out, in0=in0, scalar1=s1, scalar2=None, op0=op0),
                    reads=rd, writes=[out])
        else:
            T.issue(e, lambda: V(e).tensor_scalar(out=out, in0=in0, scalar1=s1, scalar2=s2, op0=op0, op1=op1),
                    reads=rd, writes=[out])

    def stt(out, in0, scalar, in1, op0, op1, e='dve'):
        rd = [in0, in1] + ([] if isinstance(scalar, float) else [scalar])
        T.issue(e, lambda: V(e).scalar_tensor_tensor(out=out, in0=in0, scalar=scalar, in1=in1, op0=op0, op1=op1),
                reads=rd, writes=[out])

    def cp(out, in_, e='dve'):
        T.issue(e, lambda: V(e).tensor_copy(out=out, in_=in_), reads=[in_], writes=[out])

    def mset(ap, val, e='dve'):
        T.issue(e, lambda: V(e).memset(ap, val), writes=[ap])

    def recip(out, in_):
        T.issue('dve', lambda: nc.vector.reciprocal(out=out, in_=in_), reads=[in_], writes=[out])

    def rsq(out, in_, eps_ap):
        act(out, in_, AF.Ln, bias=eps_ap)
        act(out, out, AF.Exp, bias=EPSC[:, 2:3], scale=-0.5)

    CASTS = {'win': (w_in, w_in_b, D, 256), 'wout': (w_out, w_out_b, D, 256),
             'wup': (w_up, w_up_b, D, 128), 'wdn': (w_down, w_down_b, DFF, 256)}

    def cast(nm, l, gate=None):
        src, dst, rows, step = CASTS[nm]
        pairs = [(dst[l, a:b, :], src[l, a:b, :]) for a, b in chunks(0, rows, step)]
        T.dma('pool', 'cast_%s%d' % (nm, l), pairs, wkeys=[('dr', nm, l)], rkeys=([gate] if gate else []))

    cast('win', 0)
    stage = {'B': False, 'C': False}

    T.dma('sp', 'k_cst', [(CST, cst_d)])
    T.dma('sp', 'k_ident', [(IDENT, ident_d)])
    T.dma('sp', 'k_sel', [(SELF[0:2, :], sel_d)])
    T.dma('sp', 'k_sink', [(ESKF[0:2, :], sink_d)])
    mset(ONESB, 1.0)
    mset(ONESL, 0.0)
    mset(ONESL[:, 0:64], 1.0)
    mset(ONESR, 0.0)
    mset(ONESR[:, 64:128], 1.0)
    mset(EPSC[:, 0:1], EPS * 1024.0)
    mset(EPSC[:, 1:2], EPS)
    mset(EPSC[:, 2:3], 0.0)
    cp(SELB[0:2, :], SELF[0:2, :])
    ts(G32, CST[:, 0:40], 32.0, None, ALU.mult)
    act(ESKF[0:2, :], ESKF[0:2, :], AF.Exp, bias=EPSC[0:2, 2:3])
    for l in range(2):
        cp(ESK[0:2, l, :, :], ESKF[0:2, l * 4:(l + 1) * 4].unsqueeze(2).to_broadcast([2, 4, 128]))

    ov = [OV0]

    def O32(n):
        o = ov[0]
        ov[0] += (n + 127) // 128 * 128
        assert ov[0] <= NTOT, ov[0]
        return big[:, o:o + n]

    def O16(n):
        o = ov[0]
        nf = (n + 1) // 2
        ov[0] += (nf + 127) // 128 * 128
        assert ov[0] <= NTOT, ov[0]
        return big[:, o:o + nf].bitcast(BF16)[:, 0:n]

    ov[0] = OV0
    OHs = O32(768)
    RBs = O32(8)
    Gs = O32(768)
    T.dma('sp', 'k_oh', [(OHs[0:33, :], oh_d)])
    T.dma('sp', 'k_rb', [(RBs[0:33, :], relb_d)])
    mm(D0[0:8, 0:384], RBs[0:33, 0:8], OHs[0:33, 0:384], True, True)
    mm(D1[0:8, 0:384], RBs[0:33, 0:8], OHs[0:33, 384:768], True, True)
    cp(Gs[0:8, 0:384], D0[0:8, 0:384])
    cp(Gs[0:8, 384:768], D1[0:8, 0:384])
    def scr_dma():
        T.dma('sp', 'k_scr', [(scr[kb], Gs[0:8, kb * 256:(kb + 1) * 256].unsqueeze(1).to_broadcast([8, 128, 256]))
                              for kb in range(3)], wkeys=[('dr', 'scr')])

    tab_done = [False]

    def tab_dma():
        if tab_done[0]:
            return
        tab_done[0] = True
        T.dma('sp', 'k_tab', [(TAB[:, kb, :, :], bass.AP(scr.tensor, kb * 8 * 32768 + 128, [[255, 128], [32768, 8], [1, 128]]))
                              for kb in range(3)], rkeys=[('dr', 'scr')])

    def rms_a(XSQ, a, b, bank=None):
        bank = ST if bank is None else bank
        n = b - a
        for c in range(8):
            act(XSQ[:, c % 2, 0:n], X[:, c, a:b], AF.Square)
            mm(bank[:, 0:n], ONESB, XSQ[:, c % 2, 0:n], c == 0, c == 7)

    def rms_b(H, RSTD, gcol, a, b, bank=None):
        bank = ST if bank is None else bank
        n = b - a
        rsq(RSTD[:, 0:n], bank[:, 0:n], EPSC[:, 0:1])
        for c in range(8):
            stt(H[:, c, 0:n], X[:, c, a:b], G32[:, gcol + c:gcol + c + 1], RSTD[:, 0:n], ALU.mult, ALU.mult)

    def rms(H, XSQ, RSTD, gcol, a, b):
        rms_a(XSQ, a, b)
        rms_b(H, RSTD, gcol, a, b)

    dsel = [0]
    DB6 = [D0, D1, S0, S1b, PO, PDN]
    DB2 = [D0, D1]
    dlist = [DB6]

    def dbank():
        dsel[0] += 1
        return dlist[0][dsel[0] % len(dlist[0])]

    ev = [0]

    NSLOT = 3
    xslot = [0]

    def in_prefetch(tile, a, b, s):
        which, g0, HL, W = tile
        n = b - a
        T.dma('sp', 'k_xst%d' % s, [(XST[0:n, s, :], xin[which][g0 + a:g0 + b, :])])

    def in_consume(tile, a, b, s):
        n = b - a
        for half in range(2):
            pb = dbank()
            for c4 in range(4):
                c = half * 4 + c4
                tr(pb[:, c4 * 128:c4 * 128 + n], XST[0:n, s, c * 128:(c + 1) * 128], IDENT[0:n, 0:n])
            src = pb.rearrange("p (c t) -> p c t", c=4)[:, :, 0:n]
            dst = X[:, half * 4:half * 4 + 4, a:b]
            cp(dst, src)

    def tile_body(tile):
        which, g0, HL, W = tile
        Pr = (0, W)
        R1 = (max(0, HL - 130), HL + OWN + 130)
        R2 = (max(0, HL - 1), HL + OWN + 1)
        for l, (P, R) in enumerate(((Pr, R1), (R1, R2))):
            layer(l, P, R)

    def tile_output(tile, nxt):
        which, g0, HL, W = tile
        ydst = yout[which]
        ov[0] = OV0
        XSQ = O16(2 * 512).rearrange("p (s t) -> p s t", s=2)
        RSTDS = [O32(512) for _ in range(2)]
        XNS = [O32(8 * 512).rearrange("p (c t) -> p c t", c=8) for _ in range(2)]
        blocks = chunks(0, nxt[3], 128) if nxt is not None else []
        st = {'pref': 0, 'cons': 0}

        def prefetch_upto(k):
            while st['pref'] < min(k, len(blocks)):
                j = st['pref']
                in_prefetch(nxt, blocks[j][0], blocks[j][1], j % 2)
                st['pref'] += 1

        def flush(pred):
            while st['cons'] < len(blocks) and pred(*blocks[st['cons']]):
                j = st['cons']
                prefetch_upto(j + 1)
                in_consume(nxt, blocks[j][0], blocks[j][1], j % 2)
                st['cons'] += 1
                prefetch_upto(j + 3)

        prefetch_upto(2)
        flush(lambda ia, ib: ib <= HL or ia >= HL + OWN)
        for gi2, (ga, gb) in enumerate(chunks(HL, HL + OWN, 512)):
            gn = gb - ga
            RSTD = RSTDS[gi2 % 2]
            XN = XNS[gi2 % 2]
            for c in range(8):
                act(XSQ[:, c % 2, 0:gn], X[:, c, ga:gb], AF.Square)
                mm(ST[:, 0:gn], ONESB, XSQ[:, c % 2, 0:gn], c == 0, c == 7)
            rsq(RSTD[:, 0:gn], ST[:, 0:gn], EPSC[:, 0:1])
            for c in range(8):
                stt(XN[:, c, 0:gn], X[:, c, ga:gb], G32[:, 32 + c:33 + c], RSTD[:, 0:gn], ALU.mult, ALU.mult)
            for bi, (a, b) in enumerate(chunks(ga, gb, 128)):
                n = b - a
                s = 2
                for half in range(2):
                    pb = dbank()
                    for c4 in range(4):
                        c = half * 4 + c4
                        tr(pb[0:n, c4 * 128:(c4 + 1) * 128], XN[:, c, a - ga:b - ga], IDENT)
                    act(XST[0:n, s, half * 512:(half + 1) * 512], pb[0:n, :], AF.Copy)
                T.dma('sp', 'k_xst%d' % s, [(ydst[g0 + a:g0 + b, :], XST[0:n, s, :])])
                flush(lambda ia, ib: ib <= ga)
        flush(lambda ia, ib: True)

    def layer(l, P, R):
        ov[0] = OV0
        DIAG = O16(4 * 31 * 128).rearrange("p (m j q) -> p m j q", m=4, j=31)
        WCV = O16(8 * 1024).rearrange("p (k n) -> p k n", k=8)
        HH = [O16(8 * 512).rearrange("p (c t) -> p c t", c=8)]
        XSQ = O16(2 * 512).rearrange("p (s t) -> p s t", s=2)
        RSTDS = [O32(512)]
        HH.append(O16(8 * 512).rearrange("p (c t) -> p c t", c=8))
        RSTDS.append(O32(512))
        GLU = O16(4 * WMAX).rearrange("p (c t) -> p c t", c=4)
        SIGS = [O32(512) for _ in range(2)]
        Y32s = O32(2 * 4 * 256).rearrange("p (s c t) -> p s c t", s=2, c=4)
        YBFs = O16(2 * 4 * 256).rearrange("p (s c t) -> p s c t", s=2, c=4)
        YSQs = O16(2 * 4 * 256).rearrange("p (s c t) -> p s c t", s=2, c=4)
        MEANs = O32(2 * 256).rearrange("p (s t) -> p s t", s=2)
        VARs = O32(2 * 256).rearrange("p (s t) -> p s t", s=2)
        RSCs = O32(2 * 256).rearrange("p (s t) -> p s t", s=2)
        TTs = O32(2 * 2 * 256).rearrange("p (s k t) -> p s k t", s=2, k=2)
        T.dma('sp', 'k_wcv', [(WCV, w_in_b[l].rearrange("(k p) n -> p k n", p=128)[:, :, 768:1792])],
              rkeys=[('dr', 'win', l)], wkeys=([] if stage['B'] else [('gate', 'B')]))
        if not stage['B']:
            stage['B'] = True
            cast('wout', 0, gate=('gate', 'B'))
            cast('wup', 0, gate=('gate', 'B'))
            cast('wdn', 0, gate=('gate', 'B'))
        def build_diag(m):
            col = C_CW + (l * 4 + m) * 31
            T.issue('dve', lambda: nc.vector.tensor_tensor(
                out=DIAG[:, m, :, :], in0=IDENT.unsqueeze(1).to_broadcast([128, 31, 128]),
                in1=CST[:, col:col + 31].unsqueeze(2).to_broadcast([128, 31, 128]), op=ALU.mult),
                reads=[IDENT, CST[:, col:col + 31]], writes=[DIAG[:, m, :, :]])
        Gr = (max(P[0], R[0] - 15), min(P[1], R[1] + 15))
        gch = bchunks(Gr[0], Gr[1], 512)
        rms(HH[0], XSQ, RSTDS[0], C_GA + l * 8, gch[0][0], gch[0][1])
        for ci, (a, b) in enumerate(gch):
            n = b - a
            H = HH[ci % 2]
            if ci + 1 < len(gch):
                rms(HH[(ci + 1) % 2], XSQ, RSTDS[(ci + 1) % 2], C_GA + l * 8, gch[ci + 1][0], gch[ci + 1][1])
            for m in range(4):
                SIG = SIGS[m % 2]
                pb = dbank()
                for k in range(8):
                    mm(pb[:, 0:n], WCV[:, k, 512 + m * 128:512 + (m + 1) * 128], H[:, k, 0:n], k == 0, k == 7)
                act(SIG[:, 0:n], pb[:, 0:n], AF.Sigmoid)
                pa = dbank()
                for k in range(8):
                    mm(pa[:, 0:n], WCV[:, k, m * 128:(m + 1) * 128], H[:, k, 0:n], k == 0, k == 7)
                tt(GLU[:, m, a:b], pa[:, 0:n], SIG[:, 0:n], ALU.mult)
                if ci == 0:
                    build_diag(m)
        for cci, (a, b) in enumerate(bchunks(R[0], R[1], 256)):
            n = b - a
            cs = cci % 2
            Y32, YBF, YSQ = Y32s[:, cs], YBFs[:, cs], YSQs[:, cs]
            MEAN, VAR, RSC, TT = MEANs[:, cs], VARs[:, cs], RSCs[:, cs], TTs[:, cs]
            for m in range(4):
                pb = dbank()
                taps = []
                for j in [15] + [j for j in range(31) if j != 15]:
                    sh = j - 15
                    oa = max(a, Gr[0] - sh)
                    ob = min(b, Gr[1] - sh)
                    if ob > oa:
                        taps.append((j, sh, oa, ob))
                for ti, (j, sh, oa, ob) in enumerate(taps):
                    mm(pb[:, oa - a:ob - a], DIAG[:, m, j, :], GLU[:, m, oa + sh:ob + sh], ti == 0, ti == len(taps) - 1)
                act(Y32[:, m, 0:n], pb[:, 0:n], AF.Identity, bias=CST[:, C_CB + l * 4 + m:C_CB + l * 4 + m + 1])
                cp(YBF[:, m, 0:n], Y32[:, m, 0:n])
                act(YSQ[:, m, 0:n], Y32[:, m, 0:n], AF.Square)
            for m in range(4):
                mm(ST[:, 0:n], ONESB, YBF[:, m, 0:n], m == 0, m == 3)
            for m in range(4):
                mm(ST2[:, 0:n], ONESB, YSQ[:, m, 0:n], m == 0, m == 3)
            act(MEAN[:, 0:n], ST[:, 0:n], AF.Copy, scale=1.0 / 512.0)
            tt(VAR[:, 0:n], MEAN[:, 0:n], MEAN[:, 0:n], ALU.mult)
            stt(VAR[:, 0:n], ST2[:, 0:n], 1.0 / 512.0, VAR[:, 0:n], ALU.mult, ALU.subtract)
            rsq(RSC[:, 0:n], VAR[:, 0:n], EPSC[:, 1:2])
            for m in range(4):
                tt(TT[:, m % 2, 0:n], Y32[:, m, 0:n], MEAN[:, 0:n], ALU.subtract)
                tt(TT[:, m % 2, 0:n], TT[:, m % 2, 0:n], RSC[:, 0:n], ALU.mult)
                act(CB[:, m, a:b], TT[:, m % 2, 0:n], AF.Silu,
                    bias=CST[:, C_LB + l * 4 + m:C_LB + l * 4 + m + 1],
                    scale=CST[:, C_LG + l * 4 + m:C_LG + l * 4 + m + 1])

        ov[0] = OV0
        HH = [O16(8 * 512).rearrange("p (c t) -> p c t", c=8) for _ in range(2)]
        XSQ = O16(2 * 512).rearrange("p (s t) -> p s t", s=2)
        RSTDS = [O32(512) for _ in range(2)]
        WQ = O16(8 * 768).rearrange("p (k n) -> p k n", k=8)
        WO = O16(8 * 1024).rearrange("p (k n) -> p k n", k=8)
        Q = O16(4 * WMAX).rearrange("p (c t) -> p c t", c=4)
        KT = O16(WMAX)
        NB = (WMAX + 127) // 128
        VP = O16(NB * 2 * 128).rearrange("p (b g d) -> p b g d", b=NB, g=2)
        SSB = O32(4 * 384).rearrange("p (s k q) -> p s k q", s=4, k=3)
        PT = O16(2 * 3 * 8 * 128).rearrange("p (s k h q) -> p s k h q", s=2, k=3, h=8)
        REC = O32(512)
        T.dma('sp', 'k_wq', [(WQ, w_in_b[l].rearrange("(k p) n -> p k n", p=128)[:, :, 0:768])],
              rkeys=[('dr', 'win', l)])
        T.dma('sp', 'k_wo', [(WO, w_out_b[l].rearrange("(k p) n -> p k n", p=128))],
              rkeys=[('dr', 'wout', l)], wkeys=([] if stage['C'] else [('gate', 'C')]))
        tab_dma()
        if not stage['C']:
            stage['C'] = True
            for nm in ('win', 'wout', 'wup', 'wdn'):
                cast(nm, 1, gate=('gate', 'C'))
        mset(VP, 0.0, e='pool')
        Kr = (max(P[0], R[0] - 128) // 128 * 128, min(P[1], R[1] + 128))
        pchunks = chunks(Kr[0], Kr[1], 512)

        def proj_items(ci):
            a, b = pchunks[ci]
            n = b - a
            H = HH[ci % 2]
            items = []

            def qitem(m):
                pb = dbank()
                for k in range(8):
                    mm(pb[:, 0:n], WQ[:, k, m * 128:(m + 1) * 128], H[:, k, 0:n], k == 0, k == 7)
                act(Q[:, m, a:b], pb[:, 0:n], AF.Copy, scale=0.125)

            def kitem():
                pb = dbank()
                for k in range(8):
                    mm(pb[:, 0:n], WQ[:, k, 512:640], H[:, k, 0:n], k == 0, k == 7)
                act(KT[:, a:b], pb[:, 0:n], AF.Copy)

            def vitem(ba, bb):
                nb = bb - ba
                blk = ba // 128
                pb = dbank()
                for k in range(8):
                    mm(pb[0:nb, 0:128], H[:, k, ba - a:bb - a], WQ[:, k, 640:768], k == 0, k == 7)
                act(VP[0:nb, blk, 0, 0:64], pb[0:nb, 0:64], AF.Copy)
                act(VP[0:nb, blk, 1, 64:128], pb[0:nb, 64:128], AF.Copy)

            if b > R[0] and a < R[1]:
                for m in range(4):
                    items.append(lambda m=m: qitem(m))
            items.append(kitem)
            for (ba, bb) in chunks(a, b, 128):
                items.append(lambda ba=ba, bb=bb: vitem(ba, bb))
            return items

        def wout_items(a, b):
            n = b - a

            def witem(m):
                pb = dbank()
                for k in range(8):
                    rhs = Q[:, k, a:b] if k < 4 else CB[:, k - 4, a:b]
                    mm(pb[:, 0:n], WO[:, k, m * 128:(m + 1) * 128], rhs, k == 0, k == 7)
                tt(X[:, m, a:b], pb[:, 0:n], X[:, m, a:b], ALU.add)
            return [lambda m=m: witem(m) for m in range(8)]

        dlist[0] = [D0, ST2]
        qtiles = []
        for nblk in range(R[0] // 128, (R[1] - 1) // 128 + 1):
            qa = max(R[0], nblk * 128)
            qb = min(R[1], nblk * 128 + 128)
            qtiles.append((nblk, qa, qb))
        sidx = [0]
        sbi = [0]
        SB6 = [S0, S1b, D1]
        POs = [PO, PO]
        PDs = [PDN, PDN]
        def att_S(ti):
            nblk, qa, qb = qtiles[ti]
            nq = qb - qa
            qo = qa - nblk * 128
            kbs = []
            for kb in range(3):
                kblk = nblk - 1 + kb
                k0 = kblk * 128
                if k0 < Kr[0] or k0 >= Kr[1]:
                    continue
                kbs.append((kb, kblk, min(128, Kr[1] - k0)))
            sp = sidx[0] % 2
            sidx[0] += 1
            for h4 in range(4):
                pss = []
                for g in range(2):
                    psb = SB6[sbi[0] % 3]
                    sbi[0] += 1
                    pss.append(psb[:, 0:384].rearrange("p (k q) -> p k q", k=3))
                for (kb, kblk, nk) in kbs:
                    for g in range(2):
                        mm(pss[g][0:nk, kb, 0:nq], KT[g * 64:(g + 1) * 64, kblk * 128:kblk * 128 + nk],
                           Q[g * 64:(g + 1) * 64, h4, qa:qb], True, True)
                sb0 = 2 * (h4 % 2)
                if all(nk == 128 for (_, _, nk) in kbs):
                    k0, k1 = kbs[0][0], kbs[-1][0] + 1
                    for g in range(2):
                        tt(SSB[:, sb0 + g, k0:k1, 0:nq], pss[g][:, k0:k1, 0:nq],
                           TAB[:, k0:k1, g * 4 + h4, qo:qo + nq], ALU.add)
                    act(PT[:, sp, k0:k1, h4::4, 0:nq].rearrange("p k g q -> p g k q"),
                        SSB[:, sb0:sb0 + 2, k0:k1, 0:nq], AF.Exp, bias=EPSC[:, 2:3])
                else:
                    for g in range(2):
                        h = g * 4 + h4
                        ps3 = pss[g]
                        sb = sb0 + g
                        for (kb, kblk, nk) in kbs:
                            tt(SSB[0:nk, sb, kb, 0:nq], ps3[0:nk, kb, 0:nq], TAB[0:nk, kb, h, qo:qo + nq], ALU.add)
                            act(PT[0:nk, sp, kb, h, 0:nq], SSB[0:nk, sb, kb, 0:nq], AF.Exp, bias=EPSC[0:nk, 2:3])
            return (nq, qo, kbs, sp, qa, qb)

        def att_P(st):
            nq, qo, kbs, sp, qa, qb = st
            po3 = POs[sp].rearrange("p (h q) -> p h q", h=4)[:, :, 0:nq]
            pd3 = PDs[sp].rearrange("p (h q) -> p h q", h=4)[:, :, 0:nq]
            seq = [(g, kbt) for g in range(2) for kbt in kbs]
            for i, (g, (kb, kblk, nk)) in enumerate(seq):
                mm(po3, VP[0:nk, kblk, g, :], PT[0:nk, sp, kb, g * 4:(g + 1) * 4, 0:nq], i == 0, i == len(seq) - 1)
            for i, (g, (kb, kblk, nk)) in enumerate(seq):
                mm(pd3, (ONESL if g == 0 else ONESR)[0:nk, :], PT[0:nk, sp, kb, g * 4:(g + 1) * 4, 0:nq], i == 0, False)
            mm(pd3, SELB[0:2, :], ESK[0:2, l, :, 0:nq], False, True)
            rec3 = REC.rearrange("p (h q) -> p h q", h=4)[:, :, 0:nq]
            act(rec3, pd3, AF.Ln)
            act(rec3, rec3, AF.Exp, bias=EPSC[:, 2:3], scale=-1.0)
            tt(Q[:, :, qa:qb], po3, rec3, ALU.mult)
        fill = []
        proj_done = [0]

        def rms_item(ci):
            a, b = pchunks[ci]
            return lambda: rms(HH[ci % 2], XSQ, RSTDS[ci % 2], C_GA + l * 8, a, b)

        def queue_proj_ahead():
            if proj_done[0] < len(pchunks):
                ci = proj_done[0]
                if ci == 0:
                    fill.append(('p', ci, rms_item(0)))
                if ci + 1 < len(pchunks):
                    fill.append(('p', ci, rms_item(ci + 1)))
                for it in proj_items(ci):
                    fill.append(('p', ci, it))
                proj_done[0] += 1

        def need_proj(upto_tok):
            while proj_done[0] < len(pchunks) and pchunks[proj_done[0]][0] < upto_tok:
                queue_proj_ahead()
            cmax = max([ci for ci in range(len(pchunks)) if pchunks[ci][0] < upto_tok] + [-1])
            while any(f[0] == 'p' and f[1] <= cmax for f in fill):
                fill.pop(0)[2]()

        def run_fill(k):
            for _ in range(k):
                if fill:
                    fill.pop(0)[2]()

        wch = bchunks(R[0], R[1], 512)
        wdone = [0]
        need_proj(min(Kr[1], qtiles[0][0] * 128 + 256))
        stS = att_S(0)
        for ti in range(len(qtiles)):
            if ti + 1 < len(qtiles):
                need_proj(min(Kr[1], qtiles[ti + 1][0] * 128 + 256))
                nxt = att_S(ti + 1)
            else:
                nxt = None
            if not fill:
                queue_proj_ahead()
            run_fill(4)
            att_P(stS)
            while wdone[0] < len(wch) and wch[wdone[0]][1] <= qtiles[ti][2]:
                for it in wout_items(*wch[wdone[0]]):
                    fill.append(('w', -1, it))
                wdone[0] += 1
            run_fill(3)
            stS = nxt
        while proj_done[0] < len(pchunks):
            queue_proj_ahead()
        while wdone[0] < len(wch):
            for it in wout_items(*wch[wdone[0]]):
                fill.append(('w', -1, it))
            wdone[0] += 1
        run_fill(len(fill))

        ov[0] = OV0
        WU = []
        WD = []
        for _ in range(2):
            WU.append(O16(8 * 2 * GMAX * 128).rearrange("p (k n) -> p k n", k=8))
            WD.append(O16(GMAX * 1024).rearrange("p (j n) -> p j n", j=GMAX))
        HFW = OWN + HALO + 130
        HF = O16(8 * HFW).rearrange("p (c t) -> p c t", c=8)
        XSQ = O16(2 * 512).rearrange("p (s t) -> p s t", s=2)
        RSTD = O32(512)
        U1 = O32(3 * 512).rearrange("p (s t) -> p s t", s=3)
        U2 = O32(3 * 512).rearrange("p (s t) -> p s t", s=3)
        SS = O32(2 * 512).rearrange("p (s t) -> p s t", s=2)
        GB = O16(2 * GMAX * 512).rearrange("p (s j t) -> p s j t", s=2, j=GMAX)
        fpre = bchunks(R[0], R[1], 512)
        fbanks = [ST, ST2, D0, D1]
        RSTDF = [RSTD, U1[:, 0, :], U1[:, 1, :], U1[:, 2, :]]
        for i, (a, b) in enumerate(fpre):
            rms_a(XSQ, a, b, bank=fbanks[i % 4])
        for i, (a, b) in enumerate(fpre):
            rms_b(HF[:, :, a:b], RSTDF[i % 4], C_GF + l * 8, a, b, bank=fbanks[i % 4])
        fch = bchunks(R[0], R[1], 510)
        wupv = w_up_b[l].rearrange("(k p) n -> p k n", p=128)
        wdnv = w_down_b[l].rearrange("(j p) n -> p j n", p=128)
        DB8 = [D0, D1, S0, S1b, PO, PDN, ST, ST2]
        dlist[0] = DB8

        def ff_load(gi):
            j0, G = FF_GROUPS[gi]
            s = gi % 2
            T.dma('sp', 'k_ff%d' % s,
                  [(WU[s][:, :, 0:G * 128], wupv[:, :, j0 * 128:(j0 + G) * 128]),
                   (WU[s][:, :, GMAX * 128:GMAX * 128 + G * 128], wupv[:, :, (NFF + j0) * 128:(NFF + j0 + G) * 128]),
                   (WD[s][:, 0:G, :], wdnv[:, j0:j0 + G, :])],
                  rkeys=[('dr', 'wup', l), ('dr', 'wdn', l)])

        def ff_up(gi, ci, si):
            j0, G = FF_GROUPS[gi]
            s = gi % 2
            a, b = fch[ci]
            n = b - a
            ua = max(R[0], a - 1)
            ub = min(R[1], b + 1)
            nu = ub - ua
            gs = si % 2
            for jj in range(G):
                j = j0 + jj
                us = ev[0] % 3
                ev[0] += 1
                for half, (UU, colbase) in enumerate(((U1, jj * 128), (U2, GMAX * 128 + jj * 128))):
                    jc = j if half == 0 else NFF + j
                    pb = dbank()
                    for k in range(8):
                        mm(pb[:, 0:nu], WU[s][:, k, colbase:colbase + 128], HF[:, k, ua:ub], k == 0, k == 7)
                    wc = C_FW + (l * 44 + jc) * 3
                    act(UU[:, us, 0:n], pb[:, a - ua:a - ua + n], AF.Identity,
                        bias=CST[:, C_FB + l * 44 + jc:C_FB + l * 44 + jc + 1], scale=CST[:, wc + 1:wc + 2])
                    lo = max(a, ua + 1) - a
                    if n > lo:
                        stt(UU[:, us, lo:n], pb[:, a + lo - 1 - ua:a + n - 1 - ua], CST[:, wc:wc + 1],
                            UU[:, us, lo:n], ALU.mult, ALU.add)
                    hi = min(b, ub - 1) - a
                    if hi > 0:
                        stt(UU[:, us, 0:hi], pb[:, a + 1 - ua:a + hi + 1 - ua], CST[:, wc + 2:wc + 3],
                            UU[:, us, 0:hi], ALU.mult, ALU.add)
                act(SS[:, us % 2, 0:n], U1[:, us, 0:n], AF.Silu)
                tt(GB[:, gs, jj, 0:n], SS[:, us % 2, 0:n], U2[:, us, 0:n], ALU.mult, e='pool')

        def ff_down(gi, ci, si):
            j0, G = FF_GROUPS[gi]
            s = gi % 2
            a, b = fch[ci]
            n = b - a
            gs = si % 2
            for m in range(8):
                pb = dbank()
                for jj in range(G):
                    mm(pb[:, 0:n], WD[s][:, jj, m * 128:(m + 1) * 128], GB[:, gs, jj, 0:n], jj == 0, jj == G - 1)
                tt(X[:, m, a:b], pb[:, 0:n], X[:, m, a:b], ALU.add)

        steps = [(gi, ci) for gi in range(len(FF_GROUPS)) for ci in range(len(fch))]
        prev = None
        for si, (gi, ci) in enumerate(steps):
            if ci == 0:
                ff_load(gi)
            ff_up(gi, ci, si)
            if prev is not None:
                ff_down(*prev)
            prev = (gi, ci, si)
        ff_down(*prev)
        dlist[0] = DB6

    tiles = []
    for which, L in (('p', LP), ('s', LS)):
        for t in range(L // OWN):
            HL = 0 if t == 0 else HALO
            tiles.append((which, t * OWN - HL, HL, HL + OWN + HALO))
    b0 = chunks(0, tiles[0][3], 128)
    for j in range(min(2, len(b0))):
        in_prefetch(tiles[0], b0[j][0], b0[j][1], j % 2)
    for j in range(len(b0)):
        in_consume(tiles[0], b0[j][0], b0[j][1], j % 2)
        if j + 2 < len(b0):
            in_prefetch(tiles[0], b0[j + 2][0], b0[j + 2][1], j % 2)
    scr_dma()
    for ti, tile in enumerate(tiles):
        tile_body(tile)
        tile_output(tile, tiles[ti + 1] if ti + 1 < len(tiles) else None)
    for s in range(NSLOT):
        k = 'k_xst%d' % s
        nc.sync.wait_ge(T.sem[k], T.cnt[k])
    return nc, T


_CACHE = {}


def _perm_q():
    idx = []
    for c in range(4):
        idx += list(range(c * 64, c * 64 + 64)) + list(range((c + 4) * 64, (c + 4) * 64 + 64))
    return np.array(idx)


def _buckets(rel):
    nb = 16
    ret = (rel > 0).astype(np.int64) * nb
    n = np.abs(rel)
    max_exact = 8
    nf = np.maximum(n, 1).astype(np.float32)
    large = max_exact + (np.log(nf / np.float32(max_exact)) / np.float32(np.log(128 / max_exact))
                         * np.float32(nb - max_exact)).astype(np.int32)
    large = np.minimum(large, nb - 1)
    return ret + np.where(n < max_exact, n, large)


def _onehot(mirror):
    oh = np.zeros((33, 768), np.float32)
    for kb in range(3):
        m = np.arange(256)
        rel = kb * 128 - m
        if mirror:
            relb = -rel
        else:
            relb = rel
        b = _buckets(relb)
        valid = np.abs(rel) <= 128
        row = np.where(valid, b, 32)
        oh[row, kb * 256 + m] = 1.0
    return oh


def kernel(x_prompt, x_sample, rel_bias, norm_attn_g, w_in, attn_sink, conv_dw_w, conv_dw_b,
           conv_ln_g, conv_ln_b, w_out, norm_ffn_g, w_up, ffn_dw_w, ffn_dw_b, w_down, norm_final_g):
    f = lambda a: np.ascontiguousarray(np.asarray(a, dtype=np.float32))
    x_prompt, x_sample = f(x_prompt), f(x_sample)
    if 'nc' not in _CACHE:
        _CACHE['nc'] = build_program()
    nc, _ = _CACHE['nc']
    pq = _perm_q()
    w_in_p = f(w_in).copy()
    w_in_p[:, :, 0:512] = f(w_in)[:, :, pq]
    w_out_p = f(w_out).copy()
    w_out_p[:, 0:512, :] = f(w_out)[:, pq, :]
    w_up_f, w_down_f = f(w_up), f(w_down)

    def feat(v, nchunk):
        v = f(v)
        lead = v.shape[:-1]
        v = v.reshape(lead + (nchunk, 128))
        return np.moveaxis(v, -1, 0)

    def make_cst(mirror):
        cst = np.zeros((128, NCST), np.float32)
        cst[:, C_GA:C_GA + 16] = feat(norm_attn_g, 8).reshape(128, 16)
        cst[:, C_GF:C_GF + 16] = feat(norm_ffn_g, 8).reshape(128, 16)
        cst[:, C_GN:C_GN + 8] = feat(norm_final_g, 8).reshape(128, 8)
        cst[:, C_CB:C_CB + 8] = feat(conv_dw_b, 4).reshape(128, 8)
        cst[:, C_LG:C_LG + 8] = feat(conv_ln_g, 4).reshape(128, 8)
        cst[:, C_LB:C_LB + 8] = feat(conv_ln_b, 4).reshape(128, 8)
        cw = f(conv_dw_w)
        fw = f(ffn_dw_w)
        if mirror:
            cw = cw[:, ::-1, :]
            fw = fw[:, ::-1, :]
        cwt = feat(cw, 4)
        cst[:, C_CW:C_CW + 248] = np.transpose(cwt, (0, 1, 3, 2)).reshape(128, 248)
        fwt = feat(fw, 44)
        cst[:, C_FW:C_FW + 264] = np.transpose(fwt, (0, 1, 3, 2)).reshape(128, 264)
        cst[:, C_FB:C_FB + 88] = feat(ffn_dw_b, 44).reshape(128, 88)
        return np.ascontiguousarray(cst)

    ident = np.eye(128, dtype=np.float32)
    sel = np.zeros((2, 128), np.float32)
    sel[0, 0:64] = 1.0
    sel[1, 64:128] = 1.0
    relb = np.concatenate([f(rel_bias), np.full((1, 8), NEG, np.float32)], axis=0)
    sk = f(attn_sink)
    sink = np.ascontiguousarray(np.transpose(sk.reshape(2, 2, 4), (1, 0, 2)).reshape(2, 8))
    csts = [make_cst(False), make_cst(True)]
    ohs = [_onehot(False), _onehot(True)]
    in_maps = []
    for c in range(8):
        b, side = c // 2, c % 2
        xp = x_prompt[b]
        xs = x_sample[b]
        if side:
            xp = xp[::-1]
            xs = xs[::-1]
        in_maps.append({
            "xp": np.ascontiguousarray(xp[0:LP + HALO]), "xs": np.ascontiguousarray(xs[0:LS + HALO]),
            "w_in": w_in_p, "w_out": w_out_p, "w_up": w_up_f, "w_down": w_down_f,
            "cst": csts[side], "ident": ident, "sel": sel, "oh": ohs[side], "relb": relb, "sink": sink,
        })
    res = run_bass_kernel_spmd(nc, in_maps, core_ids=list(range(8)))
    yp = np.empty((4, 4096, D), np.float32)
    ys = np.empty((4, 8192, D), np.float32)
    for c in range(8):
        b, side = c // 2, c % 2
        rp = res.results[c]["yp"]
        rs = res.results[c]["ys"]
        if side:
            yp[b, 2048:] = rp[::-1]
            ys[b, 4096:] = rs[::-1]
        else:
            yp[b, :2048] = rp
            ys[b, :4096] = rs
    return yp, ys
```

```python
import numpy as np
import concourse.bass as bass
import concourse.mybir as mybir
from concourse.bass_utils import run_bass_kernel_spmd

F32 = mybir.dt.float32
BF16 = mybir.dt.bfloat16
AF = mybir.ActivationFunctionType
ALU = mybir.AluOpType

D = 1024
DFF = 2816
NFF = 22
PW = 1792
OWN = 1024
HALO = 258
WMAX = OWN + 2 * HALO
EPS = 1e-6
NEG = -30000.0
LP = 2048
LS = 4096
NTOT = 53200
FF_GROUPS = [(0, 5), (5, 5), (10, 4), (14, 4), (18, 4)]
GMAX = 5

C_GA = 0
C_GF = 16
C_GN = 32
C_CB = 40
C_LG = 48
C_LB = 56
C_CW = 64
C_FW = 312
C_FB = 576
NCST = 664


class Trk:
    def __init__(self, nc, dram_names, psum_names):
        self.nc = nc
        self.eng = {'pe': nc.tensor, 'act': nc.scalar, 'dve': nc.vector, 'pool': nc.gpsimd, 'sp': nc.sync}
        self.sem = {}
        self.cnt = {}
        for e in ['pe', 'act', 'dve', 'pool']:
            self.sem[e] = nc.alloc_semaphore('s_' + e)
            self.cnt[e] = 0
        self.waited = {}
        self.lastw = {}
        self.readers = {}
        self.dram_names = dram_names
        self.psum_names = psum_names
        self.ninst = 0

    def units(self, ap):
        name = ap.tensor.name
        if name in self.psum_names:
            return [('ps', name)]
        if name in self.dram_names:
            return []
        es = 4 if ap.dtype == F32 else 2
        col = (ap.offset * es) % (NTOT * 4)
        ext = 1
        for st, cn in ap.ap[1:]:
            ext += (cn - 1) * abs(st)
        lo = col
        hi = col + ext * es
        return [(name, u) for u in range(lo // 512, (hi - 1) // 512 + 1)]

    def _need(self, e, ru, wu):
        need = {}
        for u in ru:
            lw = self.lastw.get(u)
            if lw:
                need[lw[0]] = max(need.get(lw[0], 0), lw[1])
        for u in wu:
            lw = self.lastw.get(u)
            if lw and lw[0] != e:
                need[lw[0]] = max(need.get(lw[0], 0), lw[1])
            for p, c in self.readers.get(u, {}).items():
                if p != e:
                    need[p] = max(need.get(p, 0), c)
        for p, c in need.items():
            if self.waited.get((e, p), 0) < c:
                self.eng[e].wait_ge(self.sem[p], c)
                self.waited[(e, p)] = c

    def issue(self, e, fn, reads=(), writes=(), rkeys=(), wkeys=()):
        ru = [u for a in reads for u in self.units(a)] + list(rkeys)
        wu = [u for a in writes for u in self.units(a)] + list(wkeys)
        self._need(e, ru, wu)
        inst = fn()
        self.cnt[e] += 1
        c = self.cnt[e]
        inst.then_inc(self.sem[e], 1)
        self.ninst += 1
        for u in ru:
            self.readers.setdefault(u, {})[e] = c
        for u in wu:
            self.lastw[u] = (e, c)
            self.readers[u] = {}

    def dma(self, q, key, pairs, rkeys=(), wkeys=()):
        if key not in self.sem:
            self.sem[key] = self.nc.alloc_semaphore('d_' + str(len(self.sem)))
            self.cnt[key] = 0
        ru = list(rkeys)
        wu = list(wkeys)
        for o, i in pairs:
            ru += self.units(i)
            wu += self.units(o)
        self._need(q, ru, wu)
        for o, i in pairs:
            self.eng[q].dma_start(out=o, in_=i).then_inc(self.sem[key], 16)
            self.cnt[key] += 16
            self.ninst += 1
        c = self.cnt[key]
        for u in ru:
            self.readers.setdefault(u, {})[key] = c
        for u in wu:
            self.lastw[u] = (key, c)
            self.readers[u] = {}


def chunks(lo, hi, step):
    out = []
    a = lo
    while a < hi:
        out.append((a, min(hi, a + step)))
        a += step
    return out


def bchunks(lo, hi, maxstep, mult=2):
    n = hi - lo
    k = (n + maxstep - 1) // maxstep
    step = (n + k - 1) // k
    step = min(maxstep, (step + mult - 1) // mult * mult)
    return chunks(lo, hi, step)


def build_program():
    nc = bass.Bass("TRN2", target_bir_lowering=False)
    dn = set()

    def dram(name, shape, dt, kind):
        dn.add(name)
        return nc.dram_tensor(name, shape, dt, kind=kind).ap()

    xin = {'p': dram("xp", [LP + HALO, D], F32, "ExternalInput"),
           's': dram("xs", [LS + HALO, D], F32, "ExternalInput")}
    yout = {'p': dram("yp", [LP, D], F32, "ExternalOutput"),
            's': dram("ys", [LS, D], F32, "ExternalOutput")}
    w_in = dram("w_in", [2, D, PW], F32, "ExternalInput")
    w_out = dram("w_out", [2, D, D], F32, "ExternalInput")
    w_up = dram("w_up", [2, D, 2 * DFF], F32, "ExternalInput")
    w_down = dram("w_down", [2, DFF, D], F32, "ExternalInput")
    cst_d = dram("cst", [128, NCST], F32, "ExternalInput")
    ident_d = dram("ident", [128, 128], F32, "ExternalInput")
    sel_d = dram("sel", [2, 128], F32, "ExternalInput")
    oh_d = dram("oh", [33, 768], F32, "ExternalInput")
    relb_d = dram("relb", [33, 8], F32, "ExternalInput")
    sink_d = dram("sink", [2, 8], F32, "ExternalInput")
    w_in_b = dram("w_in_b", [2, D, PW], BF16, "Internal")
    w_out_b = dram("w_out_b", [2, D, D], BF16, "Internal")
    w_up_b = dram("w_up_b", [2, D, 2 * DFF], BF16, "Internal")
    w_down_b = dram("w_down_b", [2, DFF, D], BF16, "Internal")
    scr = dram("scr", [3, 8, 128, 256], F32, "Internal")

    big = nc.alloc_sbuf_tensor("big", [128, NTOT], F32).ap()
    psn = set()
    PS = []
    for i in range(8):
        PS.append(nc.alloc_psum_tensor("ps%d" % i, [128, 512], F32).ap())
        psn.add("ps%d" % i)
    D0, D1, ST, ST2, S0, S1b, PO, PDN = PS
    T = Trk(nc, dn, psn)

    top = [0]

    def A32(n):
        o = top[0]
        top[0] += (n + 127) // 128 * 128
        assert top[0] <= NTOT, top[0]
        return big[:, o:o + n]

    def A16(n):
        o = top[0]
        nf = (n + 1) // 2
        top[0] += (nf + 127) // 128 * 128
        assert top[0] <= NTOT, top[0]
        return big[:, o:o + nf].bitcast(BF16)[:, 0:n]

    X = A32(8 * WMAX).rearrange("p (c t) -> p c t", c=8)
    TAB = A32(3 * 8 * 128).rearrange("p (k h q) -> p k h q", k=3, h=8)
    CST = A32(NCST)
    IDENT = A32(128)
    G32 = A32(40)
    EPSC = A32(8)
    ESKF = A32(8)
    SELF = A32(128)
    ONESB = A16(128)
    ONESL = A16(128)
    ONESR = A16(128)
    SELB = A16(128)
    ESK = A16(2 * 4 * 128).rearrange("p (l h q) -> p l h q", l=2, h=4)
    XST = A32(3 * 1024).rearrange("p (s f) -> p s f", s=3)
    CB = A16(4 * WMAX).rearrange("p (c t) -> p c t", c=4)
    OV0 = top[0]

    def mm(out, lhsT, rhs, start, stop):
        T.issue('pe', lambda: nc.tensor.matmul(out, lhsT, rhs, start=start, stop=stop),
                reads=[lhsT, rhs], writes=[out])

    def tr(out, in_, ident):
        T.issue('pe', lambda: nc.tensor.transpose(out, in_, ident), reads=[in_, ident], writes=[out])

    def act(out, in_, func, bias=None, scale=1.0, e='act'):
        rd = [in_]
        kw = {}
        if bias is not None:
            kw['bias'] = bias
            rd.append(bias)
        if not isinstance(scale, float):
            rd.append(scale)
        T.issue('act', lambda: nc.scalar.activation(out=out, in_=in_, func=func, scale=scale, **kw),
                reads=rd, writes=[out])

    def V(e):
        return nc.vector if e == 'dve' else nc.gpsimd

    def tt(out, in0, in1, op, e='dve'):
        T.issue(e, lambda: V(e).tensor_tensor(out=out, in0=in0, in1=in1, op=op), reads=[in0, in1], writes=[out])

    def ts(out, in0, s1, s2, op0, op1=None, e='dve'):
        rd = [in0] + [s for s in (s1, s2) if s is not None and not isinstance(s, float)]
        if op1 is None:
            T.issue(e, lambda: V(e).tensor_scalar(out=out, in0=in0, scalar1=s1, scalar2=None, op0=op0),
                    reads=rd, writes=[out])
        else:
            T.issue(e, lambda: V(e).tensor_scalar(out=out, in0=in0, scalar1=s1, scalar2=s2, op0=op0, op1=op1),
                    reads=rd, writes=[out])

    def stt(out, in0, scalar, in1, op0, op1, e='dve'):
        rd = [in0, in1] + ([] if isinstance(scalar, float) else [scalar])
        T.issue(e, lambda: V(e).scalar_tensor_tensor(out=out, in0=in0, scalar=scalar, in1=in1, op0=op0, op1=op1),
                reads=rd, writes=[out])

    def cp(out, in_, e='dve'):
        T.issue(e, lambda: V(e).tensor_copy(out=out, in_=in_), reads=[in_], writes=[out])

    def mset(ap, val, e='dve'):
        T.issue(e, lambda: V(e).memset(ap, val), writes=[ap])

    def recip(out, in_):
        T.issue('dve', lambda: nc.vector.reciprocal(out=out, in_=in_), reads=[in_], writes=[out])

    def rsq(out, in_, eps_ap):
        act(out, in_, AF.Ln, bias=eps_ap)
        act(out, out, AF.Exp, bias=EPSC[:, 2:3], scale=-0.5)

    CASTS = {'win': (w_in, w_in_b, D, 256), 'wout': (w_out, w_out_b, D, 256),
             'wup': (w_up, w_up_b, D, 128), 'wdn': (w_down, w_down_b, DFF, 256)}

    def cast(nm, l, gate=None):
        src, dst, rows, step = CASTS[nm]
        pairs = [(dst[l, a:b, :], src[l, a:b, :]) for a, b in chunks(0, rows, step)]
        T.dma('pool', 'cast_%s%d' % (nm, l), pairs, wkeys=[('dr', nm, l)], rkeys=([gate] if gate else []))

    cast('win', 0)
    stage = {'B': False, 'C': False}

    T.dma('sp', 'k_cst', [(CST, cst_d)])
    T.dma('sp', 'k_ident', [(IDENT, ident_d)])
    T.dma('sp', 'k_sel', [(SELF[0:2, :], sel_d)])
    T.dma('sp', 'k_sink', [(ESKF[0:2, :], sink_d)])
    mset(ONESB, 1.0)
    mset(ONESL, 0.0)
    mset(ONESL[:, 0:64], 1.0)
    mset(ONESR, 0.0)
    mset(ONESR[:, 64:128], 1.0)
    mset(EPSC[:, 0:1], EPS * 1024.0)
    mset(EPSC[:, 1:2], EPS)
    mset(EPSC[:, 2:3], 0.0)
    cp(SELB[0:2, :], SELF[0:2, :])
    ts(G32, CST[:, 0:40], 32.0, None, ALU.mult)
    act(ESKF[0:2, :], ESKF[0:2, :], AF.Exp, bias=EPSC[0:2, 2:3])
    for l in range(2):
        cp(ESK[0:2, l, :, :], ESKF[0:2, l * 4:(l + 1) * 4].unsqueeze(2).to_broadcast([2, 4, 128]))

    ov = [OV0]

    def O32(n):
        o = ov[0]
        ov[0] += (n + 127) // 128 * 128
        assert ov[0] <= NTOT, ov[0]
        return big[:, o:o + n]

    def O16(n):
        o = ov[0]
        nf = (n + 1) // 2
        ov[0] += (nf + 127) // 128 * 128
        assert ov[0] <= NTOT, ov[0]
        return big[:, o:o + nf].bitcast(BF16)[:, 0:n]

    ov[0] = OV0
    OHs = O32(768)
    RBs = O32(8)
    Gs = O32(768)
    T.dma('sp', 'k_oh', [(OHs[0:33, :], oh_d)])
    T.dma('sp', 'k_rb', [(RBs[0:33, :], relb_d)])
    mm(D0[0:8, 0:384], RBs[0:33, 0:8], OHs[0:33, 0:384], True, True)
    mm(D1[0:8, 0:384], RBs[0:33, 0:8], OHs[0:33, 384:768], True, True)
    cp(Gs[0:8, 0:384], D0[0:8, 0:384])
    cp(Gs[0:8, 384:768], D1[0:8, 0:384])
    def scr_dma():
        T.dma('sp', 'k_scr', [(scr[kb], Gs[0:8, kb * 256:(kb + 1) * 256].unsqueeze(1).to_broadcast([8, 128, 256]))
                              for kb in range(3)], wkeys=[('dr', 'scr')])

    tab_done = [False]

    def tab_dma():
        if tab_done[0]:
            return
        tab_done[0] = True
        T.dma('sp', 'k_tab', [(TAB[:, kb, :, :], bass.AP(scr.tensor, kb * 8 * 32768 + 128, [[255, 128], [32768, 8], [1, 128]]))
                              for kb in range(3)], rkeys=[('dr', 'scr')])

    def rms_a(XSQ, a, b, bank=None):
        bank = ST if bank is None else bank
        n = b - a
        for c in range(8):
            act(XSQ[:, c % 2, 0:n], X[:, c, a:b], AF.Square)
            mm(bank[:, 0:n], ONESB, XSQ[:, c % 2, 0:n], c == 0, c == 7)

    def rms_b(H, RSTD, gcol, a, b, bank=None):
        bank = ST if bank is None else bank
        n = b - a
        rsq(RSTD[:, 0:n], bank[:, 0:n], EPSC[:, 0:1])
        for c in range(8):
            stt(H[:, c, 0:n], X[:, c, a:b], G32[:, gcol + c:gcol + c + 1], RSTD[:, 0:n], ALU.mult, ALU.mult)

    def rms(H, XSQ, RSTD, gcol, a, b):
        rms_a(XSQ, a, b)
        rms_b(H, RSTD, gcol, a, b)

    dsel = [0]
    DB6 = [D0, D1, S0, S1b, PO, PDN]
    DB2 = [D0, D1]
    dlist = [DB6]

    def dbank():
        dsel[0] += 1
        return dlist[0][dsel[0] % len(dlist[0])]

    ev = [0]

    NSLOT = 3
    xslot = [0]

    def in_prefetch(tile, a, b, s):
        which, g0, HL, W = tile
        n = b - a
        T.dma('sp', 'k_xst%d' % s, [(XST[0:n, s, :], xin[which][g0 + a:g0 + b, :])])

    def in_consume(tile, a, b, s):
        n = b - a
        for half in range(2):
            pb = dbank()
            for c4 in range(4):
                c = half * 4 + c4
                tr(pb[:, c4 * 128:c4 * 128 + n], XST[0:n, s, c * 128:(c + 1) * 128], IDENT[0:n, 0:n])
            src = pb.rearrange("p (c t) -> p c t", c=4)[:, :, 0:n]
            dst = X[:, half * 4:half * 4 + 4, a:b]
            cp(dst, src)

    def tile_body(tile):
        which, g0, HL, W = tile
        Pr = (0, W)
        R1 = (max(0, HL - 130), HL + OWN + 130)
        R2 = (max(0, HL - 1), HL + OWN + 1)
        for l, (P, R) in enumerate(((Pr, R1), (R1, R2))):
            layer(l, P, R)

    def tile_output(tile, nxt):
        which, g0, HL, W = tile
        ydst = yout[which]
        ov[0] = OV0
        XSQ = O16(2 * 512).rearrange("p (s t) -> p s t", s=2)
        RSTDS = [O32(512) for _ in range(2)]
        XNS = [O32(8 * 512).rearrange("p (c t) -> p c t", c=8) for _ in range(2)]
        blocks = chunks(0, nxt[3], 128) if nxt is not None else []
        st = {'pref': 0, 'cons': 0}

        def prefetch_upto(k):
            while st['pref'] < min(k, len(blocks)):
                j = st['pref']
                in_prefetch(nxt, blocks[j][0], blocks[j][1], j % 2)
                st['pref'] += 1

        def flush(pred):
            while st['cons'] < len(blocks) and pred(*blocks[st['cons']]):
                j = st['cons']
                prefetch_upto(j + 1)
                in_consume(nxt, blocks[j][0], blocks[j][1], j % 2)
                st['cons'] += 1
                prefetch_upto(j + 3)

        prefetch_upto(2)
        flush(lambda ia, ib: ib <= HL or ia >= HL + OWN)
        for gi2, (ga, gb) in enumerate(chunks(HL, HL + OWN, 512)):
            gn = gb - ga
            RSTD = RSTDS[gi2 % 2]
            XN = XNS[gi2 % 2]
            for c in range(8):
                act(XSQ[:, c % 2, 0:gn], X[:, c, ga:gb], AF.Square)
                mm(ST[:, 0:gn], ONESB, XSQ[:, c % 2, 0:gn], c == 0, c == 7)
            rsq(RSTD[:, 0:gn], ST[:, 0:gn], EPSC[:, 0:1])
            for c in range(8):
                stt(XN[:, c, 0:gn], X[:, c, ga:gb], G32[:, 32 + c:33 + c], RSTD[:, 0:gn], ALU.mult, ALU.mult)
            for bi, (a, b) in enumerate(chunks(ga, gb, 128)):
                n = b - a
                s = 2
                for half in range(2):
                    pb = dbank()
                    for c4 in range(4):
                        c = half * 4 + c4
                        tr(pb[0:n, c4 * 128:(c4 + 1) * 128], XN[:, c, a - ga:b - ga], IDENT)
                    act(XST[0:n, s, half * 512:(half + 1) * 512], pb[0:n, :], AF.Copy)
                T.dma('sp', 'k_xst%d' % s, [(ydst[g0 + a:g0 + b, :], XST[0:n, s, :])])
                flush(lambda ia, ib: ib <= ga)
        flush(lambda ia, ib: True)

    def layer(l, P, R):
        ov[0] = OV0
        DIAG = O16(4 * 31 * 128).rearrange("p (m j q) -> p m j q", m=4, j=31)
        WCV = O16(8 * 1024).rearrange("p (k n) -> p k n", k=8)
        HH = [O16(8 * 512).rearrange("p (c t) -> p c t", c=8)]
        XSQ = O16(2 * 512).rearrange("p (s t) -> p s t", s=2)
        RSTDS = [O32(512)]
        HH.append(O16(8 * 512).rearrange("p (c t) -> p c t", c=8))
        RSTDS.append(O32(512))
        GLU = O16(4 * WMAX).rearrange("p (c t) -> p c t", c=4)
        SIGS = [O32(512) for _ in range(2)]
        Y32s = O32(2 * 4 * 256).rearrange("p (s c t) -> p s c t", s=2, c=4)
        YBFs = O16(2 * 4 * 256).rearrange("p (s c t) -> p s c t", s=2, c=4)
        YSQs = O16(2 * 4 * 256).rearrange("p (s c t) -> p s c t", s=2, c=4)
        MEANs = O32(2 * 256).rearrange("p (s t) -> p s t", s=2)
        VARs = O32(2 * 256).rearrange("p (s t) -> p s t", s=2)
        RSCs = O32(2 * 256).rearrange("p (s t) -> p s t", s=2)
        TTs = O32(2 * 2 * 256).rearrange("p (s k t) -> p s k t", s=2, k=2)
        T.dma('sp', 'k_wcv', [(WCV, w_in_b[l].rearrange("(k p) n -> p k n", p=128)[:, :, 768:1792])],
              rkeys=[('dr', 'win', l)], wkeys=([] if stage['B'] else [('gate', 'B')]))
        if not stage['B']:
            stage['B'] = True
            cast('wout', 0, gate=('gate', 'B'))
            cast('wup', 0, gate=('gate', 'B'))
            cast('wdn', 0, gate=('gate', 'B'))
        def build_diag(m):
            col = C_CW + (l * 4 + m) * 31
            T.issue('dve', lambda: nc.vector.tensor_tensor(
                out=DIAG[:, m, :, :], in0=IDENT.unsqueeze(1).to_broadcast([128, 31, 128]),
                in1=CST[:, col:col + 31].unsqueeze(2).to_broadcast([128, 31, 128]), op=ALU.mult),
                reads=[IDENT, CST[:, col:col + 31]], writes=[DIAG[:, m, :, :]])
        Gr = (max(P[0], R[0] - 15), min(P[1], R[1] + 15))
        gch = bchunks(Gr[0], Gr[1], 512)
        rms(HH[0], XSQ, RSTDS[0], C_GA + l * 8, gch[0][0], gch[0][1])
        for ci, (a, b) in enumerate(gch):
            n = b - a
            H = HH[ci % 2]
            if ci + 1 < len(gch):
                rms(HH[(ci + 1) % 2], XSQ, RSTDS[(ci + 1) % 2], C_GA + l * 8, gch[ci + 1][0], gch[ci + 1][1])
            for m in range(4):
                SIG = SIGS[m % 2]
                pb = dbank()
                for k in range(8):
                    mm(pb[:, 0:n], WCV[:, k, 512 + m * 128:512 + (m + 1) * 128], H[:, k, 0:n], k == 0, k == 7)
                act(SIG[:, 0:n], pb[:, 0:n], AF.Sigmoid)
                pa = dbank()
                for k in range(8):
                    mm(pa[:, 0:n], WCV[:, k, m * 128:(m + 1) * 128], H[:, k, 0:n], k == 0, k == 7)
                tt(GLU[:, m, a:b], pa[:, 0:n], SIG[:, 0:n], ALU.mult)
                if ci == 0:
                    build_diag(m)
        for cci, (a, b) in enumerate(bchunks(R[0], R[1], 256)):
            n = b - a
            cs = cci % 2
            Y32, YBF, YSQ = Y32s[:, cs], YBFs[:, cs], YSQs[:, cs]
            MEAN, VAR, RSC, TT = MEANs[:, cs], VARs[:, cs], RSCs[:, cs], TTs[:, cs]
            for m in range(4):
                pb = dbank()
                taps = []
                for j in [15] + [j for j in range(31) if j != 15]:
                    sh = j - 15
                    oa = max(a, Gr[0] - sh)
                    ob = min(b, Gr[1] - sh)
                    if ob > oa:
                        taps.append((j, sh, oa, ob))
                for ti, (j, sh, oa, ob) in enumerate(taps):
                    mm(pb[:, oa - a:ob - a], DIAG[:, m, j, :], GLU[:, m, oa + sh:ob + sh], ti == 0, ti == len(taps) - 1)
                act(Y32[:, m, 0:n], pb[:, 0:n], AF.Identity, bias=CST[:, C_CB + l * 4 + m:C_CB + l * 4 + m + 1])
                cp(YBF[:, m, 0:n], Y32[:, m, 0:n])
                act(YSQ[:, m, 0:n], Y32[:, m, 0:n], AF.Square)
            for m in range(4):
                mm(ST[:, 0:n], ONESB, YBF[:, m, 0:n], m == 0, m == 3)
            for m in range(4):
                mm(ST2[:, 0:n], ONESB, YSQ[:, m, 0:n], m == 0, m == 3)
            act(MEAN[:, 0:n], ST[:, 0:n], AF.Copy, scale=1.0 / 512.0)
            tt(VAR[:, 0:n], MEAN[:, 0:n], MEAN[:, 0:n], ALU.mult)
            stt(VAR[:, 0:n], ST2[:, 0:n], 1.0 / 512.0, VAR[:, 0:n], ALU.mult, ALU.subtract)
            rsq(RSC[:, 0:n], VAR[:, 0:n], EPSC[:, 1:2])
            for m in range(4):
                tt(TT[:, m % 2, 0:n], Y32[:, m, 0:n], MEAN[:, 0:n], ALU.subtract)
                tt(TT[:, m % 2, 0:n], TT[:, m % 2, 0:n], RSC[:, 0:n], ALU.mult)
                act(CB[:, m, a:b], TT[:, m % 2, 0:n], AF.Silu,
                    bias=CST[:, C_LB + l * 4 + m:C_LB + l * 4 + m + 1],
                    scale=CST[:, C_LG + l * 4 + m:C_LG + l * 4 + m + 1])

        ov[0] = OV0
        HH = [O16(8 * 512).rearrange("p (c t) -> p c t", c=8) for _ in range(2)]
        XSQ = O16(2 * 512).rearrange("p (s t) -> p s t", s=2)
        RSTDS = [O32(512) for _ in range(2)]
        WQ = O16(8 * 768).rearrange("p (k n) -> p k n", k=8)
        WO = O16(8 * 1024).rearrange("p (k n) -> p k n", k=8)
        Q = O16(4 * WMAX).rearrange("p (c t) -> p c t", c=4)
        KT = O16(WMAX)
        NB = (WMAX + 127) // 128
        VP = O16(NB * 2 * 128).rearrange("p (b g d) -> p b g d", b=NB, g=2)
        SSB = O32(4 * 384).rearrange("p (s k q) -> p s k q", s=4, k=3)
        PT = O16(2 * 3 * 8 * 128).rearrange("p (s k h q) -> p s k h q", s=2, k=3, h=8)
        REC = O32(512)
        T.dma('sp', 'k_wq', [(WQ, w_in_b[l].rearrange("(k p) n -> p k n", p=128)[:, :, 0:768])],
              rkeys=[('dr', 'win', l)])
        T.dma('sp', 'k_wo', [(WO, w_out_b[l].rearrange("(k p) n -> p k n", p=128))],
              rkeys=[('dr', 'wout', l)], wkeys=([] if stage['C'] else [('gate', 'C')]))
        tab_dma()
        if not stage['C']:
            stage['C'] = True
            for nm in ('win', 'wout', 'wup', 'wdn'):
                cast(nm, 1, gate=('gate', 'C'))
        mset(VP, 0.0, e='pool')
        Kr = (max(P[0], R[0] - 128) // 128 * 128, min(P[1], R[1] + 128))
        pchunks = chunks(Kr[0], Kr[1], 512)

        def proj_items(ci):
            a, b = pchunks[ci]
            n = b - a
            H = HH[ci % 2]
            items = []

            def qitem(m):
                pb = dbank()
                for k in range(8):
                    mm(pb[:, 0:n], WQ[:, k, m * 128:(m + 1) * 128], H[:, k, 0:n], k == 0, k == 7)
                act(Q[:, m, a:b], pb[:, 0:n], AF.Copy, scale=0.125)

            def kitem():
                pb = dbank()
                for k in range(8):
                    mm(pb[:, 0:n], WQ[:, k, 512:640], H[:, k, 0:n], k == 0, k == 7)
                act(KT[:, a:b], pb[:, 0:n], AF.Copy)

            def vitem(ba, bb):
                nb = bb - ba
                blk = ba // 128
                pb = dbank()
                for k in range(8):
                    mm(pb[0:nb, 0:128], H[:, k, ba - a:bb - a], WQ[:, k, 640:768], k == 0, k == 7)
                act(VP[0:nb, blk, 0, 0:64], pb[0:nb, 0:64], AF.Copy)
                act(VP[0:nb, blk, 1, 64:128], pb[0:nb, 64:128], AF.Copy)

            if b > R[0] and a < R[1]:
                for m in range(4):
                    items.append(lambda m=m: qitem(m))
            items.append(kitem)
            for (ba, bb) in chunks(a, b, 128):
                items.append(lambda ba=ba, bb=bb: vitem(ba, bb))
            return items

        def wout_items(a, b):
            n = b - a

            def witem(m):
                pb = dbank()
                for k in range(8):
                    rhs = Q[:, k, a:b] if k < 4 else CB[:, k - 4, a:b]
                    mm(pb[:, 0:n], WO[:, k, m * 128:(m + 1) * 128], rhs, k == 0, k == 7)
                tt(X[:, m, a:b], pb[:, 0:n], X[:, m, a:b], ALU.add)
            return [lambda m=m: witem(m) for m in range(8)]

        dlist[0] = [D0, ST2]
        qtiles = []
        for nblk in range(R[0] // 128, (R[1] - 1) // 128 + 1):
            qa = max(R[0], nblk * 128)
            qb = min(R[1], nblk * 128 + 128)
            qtiles.append((nblk, qa, qb))
        sidx = [0]
        sbi = [0]
        SB6 = [S0, S1b, D1]
        POs = [PO, PO]
        PDs = [PDN, PDN]
        def att_S(ti, nfill=0):
            nblk, qa, qb = qtiles[ti]
            nq = qb - qa
            qo = qa - nblk * 128
            kbs = []
            for kb in range(3):
                kblk = nblk - 1 + kb
                k0 = kblk * 128
                if k0 < Kr[0] or k0 >= Kr[1]:
                    continue
                kbs.append((kb, kblk, min(128, Kr[1] - k0)))
            sp = sidx[0] % 2
            sidx[0] += 1
            for h4 in range(4):
                pss = []
                for g in range(2):
                    psb = SB6[sbi[0] % 3]
                    sbi[0] += 1
                    pss.append(psb[:, 0:384].rearrange("p (k q) -> p k q", k=3))
                for (kb, kblk, nk) in kbs:
                    for g in range(2):
                        mm(pss[g][0:nk, kb, 0:nq], KT[g * 64:(g + 1) * 64, kblk * 128:kblk * 128 + nk],
                           Q[g * 64:(g + 1) * 64, h4, qa:qb], True, True)
                sb0 = 2 * (h4 % 2)
                if all(nk == 128 for (_, _, nk) in kbs):
                    k0, k1 = kbs[0][0], kbs[-1][0] + 1
                    for g in range(2):
                        tt(SSB[:, sb0 + g, k0:k1, 0:nq], pss[g][:, k0:k1, 0:nq],
                           TAB[:, k0:k1, g * 4 + h4, qo:qo + nq], ALU.add)
                    act(PT[:, sp, k0:k1, h4::4, 0:nq].rearrange("p k g q -> p g k q"),
                        SSB[:, sb0:sb0 + 2, k0:k1, 0:nq], AF.Exp, bias=EPSC[:, 2:3])
                else:
                    for g in range(2):
                        h = g * 4 + h4
                        ps3 = pss[g]
                        sb = sb0 + g
                        for (kb, kblk, nk) in kbs:
                            tt(SSB[0:nk, sb, kb, 0:nq], ps3[0:nk, kb, 0:nq], TAB[0:nk, kb, h, qo:qo + nq], ALU.add)
                            act(PT[0:nk, sp, kb, h, 0:nq], SSB[0:nk, sb, kb, 0:nq], AF.Exp, bias=EPSC[0:nk, 2:3])
                if nfill:
                    run_fill(1)
            return (nq, qo, kbs, sp, qa, qb)

        def att_P(st):
            nq, qo, kbs, sp, qa, qb = st
            po3 = POs[sp].rearrange("p (h q) -> p h q", h=4)[:, :, 0:nq]
            pd3 = PDs[sp].rearrange("p (h q) -> p h q", h=4)[:, :, 0:nq]
            seq = [(g, kbt) for g in range(2) for kbt in kbs]
            for i, (g, (kb, kblk, nk)) in enumerate(seq):
                mm(po3, VP[0:nk, kblk, g, :], PT[0:nk, sp, kb, g * 4:(g + 1) * 4, 0:nq], i == 0, i == len(seq) - 1)
            for i, (g, (kb, kblk, nk)) in enumerate(seq):
                mm(pd3, (ONESL if g == 0 else ONESR)[0:nk, :], PT[0:nk, sp, kb, g * 4:(g + 1) * 4, 0:nq], i == 0, False)
            mm(pd3, SELB[0:2, :], ESK[0:2, l, :, 0:nq], False, True)
            rec3 = REC.rearrange("p (h q) -> p h q", h=4)[:, :, 0:nq]
            act(rec3, pd3, AF.Ln)
            act(rec3, rec3, AF.Exp, bias=EPSC[:, 2:3], scale=-1.0)
            tt(Q[:, :, qa:qb], po3, rec3, ALU.mult)
        fill = []
        proj_done = [0]

        def rms_item(ci):
            a, b = pchunks[ci]
            return lambda: rms(HH[ci % 2], XSQ, RSTDS[ci % 2], C_GA + l * 8, a, b)

        def queue_proj_ahead():
            if proj_done[0] < len(pchunks):
                ci = proj_done[0]
                if ci == 0:
                    fill.append(('p', ci, rms_item(0)))
                if ci + 1 < len(pchunks):
                    fill.append(('p', ci, rms_item(ci + 1)))
                for it in proj_items(ci):
                    fill.append(('p', ci, it))
                proj_done[0] += 1

        def need_proj(upto_tok):
            while proj_done[0] < len(pchunks) and pchunks[proj_done[0]][0] < upto_tok:
                queue_proj_ahead()
            cmax = max([ci for ci in range(len(pchunks)) if pchunks[ci][0] < upto_tok] + [-1])
            while any(f[0] == 'p' and f[1] <= cmax for f in fill):
                fill.pop(0)[2]()

        def run_fill(k):
            for _ in range(k):
                if fill:
                    fill.pop(0)[2]()

        wch = bchunks(R[0], R[1], 512)
        wdone = [0]
        need_proj(min(Kr[1], qtiles[0][0] * 128 + 256))
        stS = att_S(0)
        for ti in range(len(qtiles)):
            if ti + 1 < len(qtiles):
                need_proj(min(Kr[1], qtiles[ti + 1][0] * 128 + 256))
                if not fill:
                    queue_proj_ahead()
                nxt = att_S(ti + 1, nfill=1)
            else:
                nxt = None
                if not fill:
                    queue_proj_ahead()
                run_fill(4)
            att_P(stS)
            while wdone[0] < len(wch) and wch[wdone[0]][1] <= qtiles[ti][2]:
                for it in wout_items(*wch[wdone[0]]):
                    fill.append(('w', -1, it))
                wdone[0] += 1
            run_fill(3)
            stS = nxt
        while proj_done[0] < len(pchunks):
            queue_proj_ahead()
        while wdone[0] < len(wch):
            for it in wout_items(*wch[wdone[0]]):
                fill.append(('w', -1, it))
            wdone[0] += 1
        run_fill(len(fill))

        ov[0] = OV0
        WU = []
        WD = []
        for _ in range(2):
            WU.append(O16(8 * 2 * GMAX * 128).rearrange("p (k n) -> p k n", k=8))
            WD.append(O16(GMAX * 1024).rearrange("p (j n) -> p j n", j=GMAX))
        HFW = OWN + HALO + 130
        HF = O16(8 * HFW).rearrange("p (c t) -> p c t", c=8)
        XSQ = O16(2 * 512).rearrange("p (s t) -> p s t", s=2)
        RSTD = O32(512)
        U1 = O32(3 * 512).rearrange("p (s t) -> p s t", s=3)
        U2 = O32(3 * 512).rearrange("p (s t) -> p s t", s=3)
        SS = O32(2 * 512).rearrange("p (s t) -> p s t", s=2)
        GB = O16(2 * GMAX * 512).rearrange("p (s j t) -> p s j t", s=2, j=GMAX)
        fpre = bchunks(R[0], R[1], 512)
        fbanks = [ST, ST2, D0, D1]
        RSTDF = [RSTD, U1[:, 0, :], U1[:, 1, :], U1[:, 2, :]]
        for i, (a, b) in enumerate(fpre):
            rms_a(XSQ, a, b, bank=fbanks[i % 4])
        for i, (a, b) in enumerate(fpre):
            rms_b(HF[:, :, a:b], RSTDF[i % 4], C_GF + l * 8, a, b, bank=fbanks[i % 4])
        fch = bchunks(R[0], R[1], 510)
        wupv = w_up_b[l].rearrange("(k p) n -> p k n", p=128)
        wdnv = w_down_b[l].rearrange("(j p) n -> p j n", p=128)
        DB8 = [D0, D1, S0, S1b, PO, PDN, ST, ST2]
        dlist[0] = DB8

        def ff_load(gi):
            j0, G = FF_GROUPS[gi]
            s = gi % 2
            T.dma('sp', 'k_ff%d' % s,
                  [(WU[s][:, :, 0:G * 128], wupv[:, :, j0 * 128:(j0 + G) * 128]),
                   (WU[s][:, :, GMAX * 128:GMAX * 128 + G * 128], wupv[:, :, (NFF + j0) * 128:(NFF + j0 + G) * 128]),
                   (WD[s][:, 0:G, :], wdnv[:, j0:j0 + G, :])],
                  rkeys=[('dr', 'wup', l), ('dr', 'wdn', l)])

        def ff_up(gi, ci, si):
            j0, G = FF_GROUPS[gi]
            s = gi % 2
            a, b = fch[ci]
            n = b - a
            ua = max(R[0], a - 1)
            ub = min(R[1], b + 1)
            nu = ub - ua
            gs = si % 2
            for jj in range(G):
                j = j0 + jj
                us = ev[0] % 3
                ev[0] += 1
                for half, (UU, colbase) in enumerate(((U1, jj * 128), (U2, GMAX * 128 + jj * 128))):
                    jc = j if half == 0 else NFF + j
                    pb = dbank()
                    for k in range(8):
                        mm(pb[:, 0:nu], WU[s][:, k, colbase:colbase + 128], HF[:, k, ua:ub], k == 0, k == 7)
                    wc = C_FW + (l * 44 + jc) * 3
                    act(UU[:, us, 0:n], pb[:, a - ua:a - ua + n], AF.Identity,
                        bias=CST[:, C_FB + l * 44 + jc:C_FB + l * 44 + jc + 1], scale=CST[:, wc + 1:wc + 2])
                    lo = max(a, ua + 1) - a
                    if n > lo:
                        stt(UU[:, us, lo:n], pb[:, a + lo - 1 - ua:a + n - 1 - ua], CST[:, wc:wc + 1],
                            UU[:, us, lo:n], ALU.mult, ALU.add)
                    hi = min(b, ub - 1) - a
                    if hi > 0:
                        stt(UU[:, us, 0:hi], pb[:, a + 1 - ua:a + hi + 1 - ua], CST[:, wc + 2:wc + 3],
                            UU[:, us, 0:hi], ALU.mult, ALU.add)
                act(SS[:, us % 2, 0:n], U1[:, us, 0:n], AF.Silu)
                tt(GB[:, gs, jj, 0:n], SS[:, us % 2, 0:n], U2[:, us, 0:n], ALU.mult, e='pool')

        def ff_down(gi, ci, si):
            j0, G = FF_GROUPS[gi]
            s = gi % 2
            a, b = fch[ci]
            n = b - a
            gs = si % 2
            for m in range(8):
                pb = dbank()
                for jj in range(G):
                    mm(pb[:, 0:n], WD[s][:, jj, m * 128:(m + 1) * 128], GB[:, gs, jj, 0:n], jj == 0, jj == G - 1)
                tt(X[:, m, a:b], pb[:, 0:n], X[:, m, a:b], ALU.add)

        steps = [(gi, ci) for gi in range(len(FF_GROUPS)) for ci in range(len(fch))]
        prev = None
        for si, (gi, ci) in enumerate(steps):
            if ci == 0:
                ff_load(gi)
            ff_up(gi, ci, si)
            if prev is not None:
                ff_down(*prev)
            prev = (gi, ci, si)
        ff_down(*prev)
        dlist[0] = DB6

    tiles = []
    for which, L in (('p', LP), ('s', LS)):
        for t in range(L // OWN):
            HL = 0 if t == 0 else HALO
            tiles.append((which, t * OWN - HL, HL, HL + OWN + HALO))
    b0 = chunks(0, tiles[0][3], 128)
    for j in range(min(2, len(b0))):
        in_prefetch(tiles[0], b0[j][0], b0[j][1], j % 2)
    for j in range(len(b0)):
        in_consume(tiles[0], b0[j][0], b0[j][1], j % 2)
        if j + 2 < len(b0):
            in_prefetch(tiles[0], b0[j + 2][0], b0[j + 2][1], j % 2)
    scr_dma()
    for ti, tile in enumerate(tiles):
        tile_body(tile)
        tile_output(tile, tiles[ti + 1] if ti + 1 < len(tiles) else None)
    for s in range(NSLOT):
        k = 'k_xst%d' % s
        nc.sync.wait_ge(T.sem[k], T.cnt[k])
    return nc, T


_CACHE = {}


def _perm_q():
    idx = []
    for c in range(4):
        idx += list(range(c * 64, c * 64 + 64)) + list(range((c + 4) * 64, (c + 4) * 64 + 64))
    return np.array(idx)


def _buckets(rel):
    nb = 16
    ret = (rel > 0).astype(np.int64) * nb
    n = np.abs(rel)
    max_exact = 8
    nf = np.maximum(n, 1).astype(np.float32)
    large = max_exact + (np.log(nf / np.float32(max_exact)) / np.float32(np.log(128 / max_exact))
                         * np.float32(nb - max_exact)).astype(np.int32)
    large = np.minimum(large, nb - 1)
    return ret + np.where(n < max_exact, n, large)


def _onehot(mirror):
    oh = np.zeros((33, 768), np.float32)
    for kb in range(3):
        m = np.arange(256)
        rel = kb * 128 - m
        if mirror:
            relb = -rel
        else:
            relb = rel
        b = _buckets(relb)
        valid = np.abs(rel) <= 128
        row = np.where(valid, b, 32)
        oh[row, kb * 256 + m] = 1.0
    return oh


def kernel(x_prompt, x_sample, rel_bias, norm_attn_g, w_in, attn_sink, conv_dw_w, conv_dw_b,
           conv_ln_g, conv_ln_b, w_out, norm_ffn_g, w_up, ffn_dw_w, ffn_dw_b, w_down, norm_final_g):
    f = lambda a: np.ascontiguousarray(np.asarray(a, dtype=np.float32))
    x_prompt, x_sample = f(x_prompt), f(x_sample)
    if 'nc' not in _CACHE:
        _CACHE['nc'] = build_program()
    nc, _ = _CACHE['nc']
    pq = _perm_q()
    w_in_p = f(w_in).copy()
    w_in_p[:, :, 0:512] = f(w_in)[:, :, pq]
    w_out_p = f(w_out).copy()
    w_out_p[:, 0:512, :] = f(w_out)[:, pq, :]
    w_up_f, w_down_f = f(w_up), f(w_down)

    def feat(v, nchunk):
        v = f(v)
        lead = v.shape[:-1]
        v = v.reshape(lead + (nchunk, 128))
        return np.moveaxis(v, -1, 0)

    def make_cst(mirror):
        cst = np.zeros((128, NCST), np.float32)
        cst[:, C_GA:C_GA + 16] = feat(norm_attn_g, 8).reshape(128, 16)
        cst[:, C_GF:C_GF + 16] = feat(norm_ffn_g, 8).reshape(128, 16)
        cst[:, C_GN:C_GN + 8] = feat(norm_final_g, 8).reshape(128, 8)
        cst[:, C_CB:C_CB + 8] = feat(conv_dw_b, 4).reshape(128, 8)
        cst[:, C_LG:C_LG + 8] = feat(conv_ln_g, 4).reshape(128, 8)
        cst[:, C_LB:C_LB + 8] = feat(conv_ln_b, 4).reshape(128, 8)
        cw = f(conv_dw_w)
        fw = f(ffn_dw_w)
        if mirror:
            cw = cw[:, ::-1, :]
            fw = fw[:, ::-1, :]
        cwt = feat(cw, 4)
        cst[:, C_CW:C_CW + 248] = np.transpose(cwt, (0, 1, 3, 2)).reshape(128, 248)
        fwt = feat(fw, 44)
        cst[:, C_FW:C_FW + 264] = np.transpose(fwt, (0, 1, 3, 2)).reshape(128, 264)
        cst[:, C_FB:C_FB + 88] = feat(ffn_dw_b, 44).reshape(128, 88)
        return np.ascontiguousarray(cst)

    ident = np.eye(128, dtype=np.float32)
    sel = np.zeros((2, 128), np.float32)
    sel[0, 0:64] = 1.0
    sel[1, 64:128] = 1.0
    relb = np.concatenate([f(rel_bias), np.full((1, 8), NEG, np.float32)], axis=0)
    sk = f(attn_sink)
    sink = np.ascontiguousarray(np.transpose(sk.reshape(2, 2, 4), (1, 0, 2)).reshape(2, 8))
    csts = [make_cst(False), make_cst(True)]
    ohs = [_onehot(False), _onehot(True)]
    in_maps = []
    for c in range(8):
        b, side = c // 2, c % 2
        xp = x_prompt[b]
        xs = x_sample[b]
        if side:
            xp = xp[::-1]
            xs = xs[::-1]
        in_maps.append({
            "xp": np.ascontiguousarray(xp[0:LP + HALO]), "xs": np.ascontiguousarray(xs[0:LS + HALO]),
            "w_in": w_in_p, "w_out": w_out_p, "w_up": w_up_f, "w_down": w_down_f,
            "cst": csts[side], "ident": ident, "sel": sel, "oh": ohs[side], "relb": relb, "sink": sink,
        })
    res = run_bass_kernel_spmd(nc, in_maps, core_ids=list(range(8)))
    yp = np.empty((4, 4096, D), np.float32)
    ys = np.empty((4, 8192, D), np.float32)
    for c in range(8):
        b, side = c // 2, c % 2
        rp = res.results[c]["yp"]
        rs = res.results[c]["ys"]
        if side:
            yp[b, 2048:] = rp[::-1]
            ys[b, 4096:] = rs[::-1]
        else:
            yp[b, :2048] = rp
            ys[b, :4096] = rs
    return yp, ys
```
